# Optimizing a Trainium2 kernel written in Bass

```python
import jax, jax.numpy as jnp
from jax import lax
import numpy as np

D_MODEL = 1024
BATCH = 8
SEQ = 2048
DEPTH = 4

GRID_W = 64
CTX_LEN = 256
ATT_HEADS = 8
ATT_KV_HEADS = 2
ATT_HEAD_DIM = 64
ATT_WIDTH = ATT_HEADS * ATT_HEAD_DIM
KV_WIDTH = ATT_KV_HEADS * ATT_HEAD_DIM
WINDOW = 128
BLOCK = 128
ROPE_BASE = 10000.0
M_HEADS = 4
M_HEAD_DIM = 128
M_WIDTH = M_HEADS * M_HEAD_DIM
CONV_K = 5
CHUNK = 128
N_DIR = 2
MIX_WIDTH = ATT_WIDTH + M_WIDTH
EPS = 1e-6
SPLIT_SIZES = (ATT_WIDTH, KV_WIDTH, KV_WIDTH, ATT_WIDTH, M_WIDTH, M_WIDTH, M_WIDTH, M_WIDTH, M_WIDTH, N_DIR * M_HEADS, N_DIR * M_HEADS)
IN_COLS = 2 * ATT_WIDTH + 2 * KV_WIDTH + 5 * M_WIDTH + 2 * N_DIR * M_HEADS

kernel_name = "hymba_style_window_gqa_bidir_mlstm_prefix_dit"


def rmsnorm(x, g):
    xf = x.astype(jnp.float32)
    y = xf * lax.rsqrt(jnp.mean(xf * xf, axis=-1, keepdims=True) + EPS)
    return (y * g.astype(jnp.float32)).astype(x.dtype)


def project(xn, w_in):
    idx = np.cumsum(SPLIT_SIZES)[:-1].tolist()
    return jnp.split(xn @ w_in, idx, axis=-1)


def axial_rope_tables(T):
    rows = T // GRID_W
    row = jnp.repeat(jnp.arange(rows), GRID_W).astype(jnp.float32)
    col = jnp.tile(jnp.arange(GRID_W), rows).astype(jnp.float32)
    half = ATT_HEAD_DIM // 2
    inv = ROPE_BASE ** (-jnp.arange(0, half, 2, dtype=jnp.float32) / half)
    ang_r = row[:, None] * inv
    ang_c = col[:, None] * inv
    return (jnp.cos(ang_r)[:, None, :], jnp.sin(ang_r)[:, None, :],
            jnp.cos(ang_c)[:, None, :], jnp.sin(ang_c)[:, None, :])


def rope_2d(u, tables):
    cos_r, sin_r, cos_c, sin_c = [t.astype(u.dtype) for t in tables]

    def rot(a, cos, sin):
        a1, a2 = jnp.split(a, 2, axis=-1)
        return jnp.concatenate([a1 * cos - a2 * sin, a2 * cos + a1 * sin], axis=-1)

    ur, uc = jnp.split(u, 2, axis=-1)
    return jnp.concatenate([rot(ur, cos_r, sin_r), rot(uc, cos_c, sin_c)], axis=-1)


def window_ctx_attention(q, k, v, kc, vc, sink):
    B, T, H, dh = q.shape
    G = H // ATT_KV_HEADS
    NB = T // BLOCK
    scale = dh ** -0.5
    qb = q.reshape(B, NB, BLOCK, ATT_KV_HEADS, G, dh)
    pad = ((0, 0), (BLOCK, BLOCK), (0, 0), (0, 0))
    kp = jnp.pad(k, pad).reshape(B, NB + 2, BLOCK, ATT_KV_HEADS, dh)
    vp = jnp.pad(v, pad).reshape(B, NB + 2, BLOCK, ATT_KV_HEADS, dh)
    kw = jnp.concatenate([kp[:, :-2], kp[:, 1:-1], kp[:, 2:]], axis=2)
    vw = jnp.concatenate([vp[:, :-2], vp[:, 1:-1], vp[:, 2:]], axis=2)
    s_win = jnp.einsum('bnqhgd,bnkhd->bhgnqk', qb, kw).astype(jnp.float32) * scale
    qpos = jnp.arange(NB)[:, None, None] * BLOCK + jnp.arange(BLOCK)[None, :, None]
    kpos = (jnp.arange(NB)[:, None, None] - 1) * BLOCK + jnp.arange(3 * BLOCK)[None, None, :]
    valid = (jnp.abs(qpos - kpos) <= WINDOW) & (kpos >= 0) & (kpos < T)
    s_win = jnp.where(valid, s_win, -jnp.inf)
    s_ctx = jnp.einsum('bnqhgd,bchd->bhgnqc', qb, kc).astype(jnp.float32) * scale
    s_sink = jnp.broadcast_to(sink.astype(jnp.float32).reshape(1, ATT_KV_HEADS, G, 1, 1, 1),
                              s_win.shape[:-1] + (1,))
    p = jax.nn.softmax(jnp.concatenate([s_win, s_ctx, s_sink], axis=-1), axis=-1)
    Lw = 3 * BLOCK
    Lc = kc.shape[1]
    p_win = p[..., :Lw].astype(v.dtype)
    p_ctx = p[..., Lw:Lw + Lc].astype(v.dtype)
    out = (jnp.einsum('bhgnqk,bnkhd->bnqhgd', p_win, vw)
           + jnp.einsum('bhgnqc,bchd->bnqhgd', p_ctx, vc))
    return out.reshape(B, T, H * dh)


def ctx_attention(qc, kc, vc, sink):
    B, Lc, H, dh = qc.shape
    G = H // ATT_KV_HEADS
    qg = qc.reshape(B, Lc, ATT_KV_HEADS, G, dh)
    s = jnp.einsum('bqhgd,bkhd->bhgqk', qg, kc).astype(jnp.float32) * dh ** -0.5
    s_sink = jnp.broadcast_to(sink.astype(jnp.float32).reshape(1, ATT_KV_HEADS, G, 1, 1),
                              s.shape[:-1] + (1,))
    p = jax.nn.softmax(jnp.concatenate([s, s_sink], axis=-1), axis=-1)
    out = jnp.einsum('bhgqk,bkhd->bqhgd', p[..., :Lc].astype(vc.dtype), vc)
    return out.reshape(B, Lc, H * dh)


def dwconv(u, w, b):
    C = u.shape[-1]
    y = lax.conv_general_dilated(u, w[:, None, :].astype(u.dtype), window_strides=(1,),
                                 padding=[(CONV_K // 2, CONV_K // 2)],
                                 dimension_numbers=('NWC', 'WIO', 'NWC'), feature_group_count=C)
    return y + b.astype(u.dtype)


def mlstm_init_state(B):
    N = N_DIR * B * M_HEADS
    return (jnp.zeros((N, M_HEAD_DIM, M_HEAD_DIM), jnp.float32),
            jnp.zeros((N, M_HEAD_DIM), jnp.float32),
            jnp.full((N,), -jnp.inf, jnp.float32))


def mlstm_chunkwise(q, k, v, i_pre, f_pre, state):
    N, T, dh = q.shape
    NC = T // CHUNK
    f32 = jnp.float32

    def chunks(u):
        return u.astype(f32).reshape((N, NC, CHUNK) + u.shape[2:]).swapaxes(0, 1)

    qc = chunks(q)
    kc = chunks(k) * dh ** -0.5
    vc = chunks(v)
    lf = chunks(jax.nn.log_sigmoid(f_pre.astype(f32)))
    ic = chunks(i_pre)
    lower = jnp.tril(jnp.ones((CHUNK, CHUNK), bool))

    def step(carry, inp):
        C, n, m = carry
        qq, kk, vv, lfc, ii = inp
        b = jnp.cumsum(lfc, axis=-1)
        dmat = jnp.where(lower, b[:, :, None] - b[:, None, :] + ii[:, None, :], -jnp.inf)
        inter = b + m[:, None]
        m_t = jnp.maximum(inter, jnp.max(dmat, axis=-1))
        w_inter = jnp.exp(inter - m_t)
        qk = jnp.einsum('ntd,nsd->nts', qq, kk) * jnp.exp(dmat - m_t[:, :, None])
        num = (w_inter[..., None] * jnp.einsum('nvd,ntd->ntv', C, qq)
               + jnp.einsum('nts,nsv->ntv', qk, vv))
        den = w_inter * jnp.einsum('nd,ntd->nt', n, qq) + qk.sum(-1)
        h = num / jnp.maximum(jnp.abs(den), jnp.exp(-m_t))[..., None]
        bL = b[:, -1]
        g = bL[:, None] - b + ii
        m_new = jnp.maximum(bL + m, jnp.max(g, axis=-1))
        a = jnp.exp(bL + m - m_new)
        w = jnp.exp(g - m_new[:, None])
        C_new = a[:, None, None] * C + jnp.einsum('ns,nsv,nsd->nvd', w, vv, kk)
        n_new = a[:, None] * n + jnp.einsum('ns,nsd->nd', w, kk)
        return (C_new, n_new, m_new), h

    state, h = lax.scan(step, state, (qc, kc, vc, lf, ic))
    h = h.swapaxes(0, 1).reshape(N, T, dh)
    return h.astype(q.dtype), state


def mlstm_bidir(q, k, v, i_pre, f_pre, state):
    B, T, _ = q.shape

    def heads(u):
        u = u.reshape(B, T, M_HEADS, M_HEAD_DIM).transpose(0, 2, 1, 3)
        return jnp.stack([u, jnp.flip(u, axis=2)]).reshape(N_DIR * B * M_HEADS, T, M_HEAD_DIM)

    def gates(gt):
        gt = gt.reshape(B, T, N_DIR, M_HEADS).transpose(2, 0, 3, 1)
        return jnp.stack([gt[0], jnp.flip(gt[1], axis=-1)]).reshape(N_DIR * B * M_HEADS, T)

    h, state = mlstm_chunkwise(heads(q), heads(k), heads(v), gates(i_pre), gates(f_pre), state)
    h = h.reshape(N_DIR, B, M_HEADS, T, M_HEAD_DIM)
    h = h[0] + jnp.flip(h[1], axis=2)
    return h.transpose(0, 2, 1, 3).reshape(B, T, M_WIDTH), state


def mlstm_post(h, o_pre, z, head_g):
    B, T, _ = h.shape
    h = jax.nn.sigmoid(o_pre) * h
    hh = rmsnorm(h.reshape(B, T, M_HEADS, M_HEAD_DIM), head_g.reshape(M_HEADS, M_HEAD_DIM))
    return hh.reshape(B, T, M_WIDTH) * jax.nn.silu(z)


def mlstm_qk(mq, mk, conv_w, conv_b):
    qk = jax.nn.silu(dwconv(jnp.concatenate([mq, mk], axis=-1), conv_w, conv_b))
    return jnp.split(qk, 2, axis=-1)


def hybrid_layer(x, ctx, mod_x, mod_c, tables, norm_g, w_in, conv_w, conv_b, gate_b, sink,
                 head_g, w_out, last):
    shift_x, scale_x, gate_x = mod_x
    shift_c, scale_c, gate_c = mod_c
    B, T, _ = x.shape
    Lc = ctx.shape[1]
    b_i, b_f = jnp.split(gate_b, 2)
    xn = rmsnorm(x, norm_g) * (1 + scale_x) + shift_x
    cn = rmsnorm(ctx, norm_g) * (1 + scale_c) + shift_c
    aq, ak, av, az, mq, mk, mv, mo, mz, mi, mf = project(xn, w_in)
    caq, cak, cav, caz, cmq, cmk, cmv, cmo, cmz, cmi, cmf = project(cn, w_in)

    q = rope_2d(aq.reshape(B, T, ATT_HEADS, ATT_HEAD_DIM), tables)
    k = rope_2d(ak.reshape(B, T, ATT_KV_HEADS, ATT_HEAD_DIM), tables)
    v = av.reshape(B, T, ATT_KV_HEADS, ATT_HEAD_DIM)
    kc = cak.reshape(B, Lc, ATT_KV_HEADS, ATT_HEAD_DIM)
    vc = cav.reshape(B, Lc, ATT_KV_HEADS, ATT_HEAD_DIM)
    att_x = window_ctx_attention(q, k, v, kc, vc, sink) * jax.nn.silu(az)

    cq, ck = mlstm_qk(cmq, cmk, conv_w, conv_b)
    xq, xk = mlstm_qk(mq, mk, conv_w, conv_b)
    h_ctx, ctx_state = mlstm_bidir(cq, ck, cmv, cmi + b_i, cmf + b_f, mlstm_init_state(B))
    h_lat, _ = mlstm_bidir(xq, xk, mv, mi + b_i, mf + b_f, ctx_state)
    m_x = mlstm_post(h_lat, mo, mz, head_g)

    out_x = jnp.concatenate([att_x, m_x], axis=-1) @ w_out
    x = x + gate_x * out_x
    if not last:
        qc = caq.reshape(B, Lc, ATT_HEADS, ATT_HEAD_DIM)
        att_c = ctx_attention(qc, kc, vc, sink) * jax.nn.silu(caz)
        m_c = mlstm_post(h_ctx, cmo, cmz, head_g)
        out_c = jnp.concatenate([att_c, m_c], axis=-1) @ w_out
        ctx = ctx + gate_c * out_c
    return x, ctx


def setup_inputs(seed: int = 0) -> dict:
    key = jax.random.key(seed)
    ks = jax.random.split(key, 16)
    f32 = jnp.float32
    nrm = lambda k, s: jax.random.normal(k, s, f32)
    f_bias = jnp.tile(jnp.linspace(3.0, 6.0, M_HEADS), N_DIR)
    gate_b = jnp.concatenate([0.1 * nrm(ks[8], (DEPTH, N_DIR * M_HEADS)),
                              f_bias[None, :] + 0.1 * nrm(ks[9], (DEPTH, N_DIR * M_HEADS))], axis=-1)
    return {
        "x": nrm(ks[0], (BATCH, SEQ, D_MODEL)),
        "c": nrm(ks[1], (BATCH, D_MODEL)),
        "ctx": nrm(ks[2], (BATCH, CTX_LEN, D_MODEL)),
        "c_ctx": nrm(ks[3], (D_MODEL,)),
        "w_ada": 0.5 * D_MODEL ** -0.5 * nrm(ks[4], (DEPTH, D_MODEL, 3 * D_MODEL)),
        "b_ada": 0.01 * nrm(ks[5], (DEPTH, 3 * D_MODEL)),
        "norm_g": 1.0 + 0.02 * nrm(ks[6], (DEPTH, D_MODEL)),
        "w_in": D_MODEL ** -0.5 * nrm(ks[7], (DEPTH, D_MODEL, IN_COLS)),
        "conv_w": CONV_K ** -0.5 * nrm(ks[10], (DEPTH, CONV_K, 2 * M_WIDTH)),
        "conv_b": 0.01 * nrm(ks[11], (DEPTH, 2 * M_WIDTH)),
        "gate_b": gate_b,
        "sink": 0.5 * nrm(ks[12], (DEPTH, ATT_HEADS)),
        "head_g": 1.0 + 0.02 * nrm(ks[13], (DEPTH, M_WIDTH)),
        "w_out": MIX_WIDTH ** -0.5 * nrm(ks[14], (DEPTH, MIX_WIDTH, D_MODEL)),
        "final_g": 1.0 + 0.02 * nrm(ks[15], (D_MODEL,)),
    }


def reference(x, c, ctx, c_ctx, w_ada, b_ada, norm_g, w_in, conv_w, conv_b, gate_b, sink,
              head_g, w_out, final_g):
    T = x.shape[1]
    tables = axial_rope_tables(T)
    sc = jax.nn.silu(c)
    scc = jax.nn.silu(c_ctx)
    for l in range(DEPTH):
        mod_x = jnp.split((sc @ w_ada[l] + b_ada[l])[:, None, :], 3, axis=-1)
        mod_c = jnp.split(scc @ w_ada[l] + b_ada[l], 3, axis=-1)
        x, ctx = hybrid_layer(x, ctx, mod_x, mod_c, tables, norm_g[l], w_in[l], conv_w[l],
                              conv_b[l], gate_b[l], sink[l], head_g[l], w_out[l],
                              last=(l == DEPTH - 1))
    return rmsnorm(x, final_g)
```

```python
import math
import numpy as np
import concourse.bass as bass
import concourse.mybir as mybir
from concourse.bass_utils import run_bass_kernel_spmd

F32 = mybir.dt.float32
BF16 = mybir.dt.bfloat16
AF = mybir.ActivationFunctionType
ALU = mybir.AluOpType
AX = mybir.AxisListType

L = 4
D = 1024
T = 2048
LC = 256
U = T + LC
NCH = U // 128
KC = 8
EPS = 1e-6
GROUPS = [(0, 256), (256, 512), (768, 512), (1280, 512), (1792, 512)]
ORDER = [list(range(NCH)), [1, 0] + list(range(NCH - 1, 1, -1))]
NFM = 22

ENG = ['pe', 'act', 'dve', 'pool', 'sp']
SAME_ENGINE_SYNC = {'act', 'dve', 'pool'}


def _size(dt):
    return 2 if dt == BF16 else 4


class Prog:
    def __init__(self):
        self.nc = bass.Bass("TRN2", target_bir_lowering=False)
        self.ops = {e: [] for e in ENG}
        self.seen = {e: {} for e in ENG}
        self.lastw = {}
        self.readers = {}
        self.sb_off = 16512
        self.sb_max = 0
        self.sems = []
        self.semval = []
        self.esem = {}
        for e in ENG:
            self.esem[e] = self._newsem("e_" + e)
        self.esem['pe_tr'] = self._newsem("e_pe_tr")
        self.dq = {'sp': [self._newsem("dsp%d" % i) for i in range(10)],
                   'pool': [self._newsem("dpl%d" % i) for i in range(6)],
                   'act': [self._newsem("dac%d" % i) for i in range(2)]}
        self.drr = {'sp': 0, 'pool': 0, 'act': 0}
        self.ps = [self.nc.alloc_psum_tensor("psb%d" % i, [128, 512], F32) for i in range(8)]
        self.psrr = 0
        self.nm = 0
        self.marks = []

    def _newsem(self, name):
        h = self.nc.alloc_semaphore(name=name)
        self.sems.append(h)
        self.semval.append(0)
        return len(self.sems) - 1

    def mark(self, label):
        self.marks.append((label, {e: len(self.ops[e]) for e in ENG}))

    def new_epoch(self):
        for e in ENG:
            self.esem[e] = self._newsem("e%d_%s" % (len(self.sems), e))
        self.esem['pe_tr'] = self._newsem("e%d_pe_tr" % len(self.sems))

    def sb(self, name, shape, dt):
        nbytes = int(np.prod(shape[1:])) * _size(dt)
        off = (self.sb_off + 31) // 32 * 32
        self.sb_off = off + nbytes
        self.sb_max = max(self.sb_max, self.sb_off)
        assert self.sb_off <= 229344, ("SBUF overflow", name, self.sb_off)
        self.nm += 1
        return self.nc.alloc_sbuf_tensor_at("%s_%d" % (name, self.nm), list(shape), dt, offset=off)

    def bank(self):
        i = self.psrr
        self.psrr = (self.psrr + 1) % 8
        return i

    def _deps(self, eng, reads, writes):
        deps = {}

        def add(tok):
            if tok is None:
                return
            s, v = tok
            if deps.get(s, 0) < v:
                deps[s] = v
        for k in reads:
            add(self.lastw.get(k))
        for k in writes:
            add(self.lastw.get(k))
            for r in self.readers.get(k, ()):
                add(r)
        waits = []
        own = self.esem[eng]
        for s, v in deps.items():
            if s == own and eng not in SAME_ENGINE_SYNC:
                continue
            if self.seen[eng].get(s, 0) < v:
                self.seen[eng][s] = v
                waits.append((s, v))
        return waits

    def _commit(self, tok, reads, writes):
        for k in reads:
            self.readers.setdefault(k, []).append(tok)
        for k in writes:
            self.lastw[k] = tok
            self.readers[k] = []

    def op(self, eng, fn, reads=(), writes=(), cls=None):
        waits = self._deps(eng, reads, writes)
        s = self.esem[eng if cls is None else cls]
        self.semval[s] += 1
        tok = (s, self.semval[s])
        self.ops[eng].append((waits, fn, s, 1))
        self._commit(tok, reads, writes)

    def dma(self, q, out, in_, reads=(), writes=()):
        waits = self._deps(q, reads, writes)
        lst = self.dq[q]
        s = lst[self.drr[q] % len(lst)]
        self.drr[q] += 1
        prev = self.semval[s]
        if prev > 0 and self.seen[q].get(s, 0) < prev:
            self.seen[q][s] = prev
            waits.append((s, prev))
        self.semval[s] += 16
        tok = (s, self.semval[s])
        self.ops[q].append((waits, lambda e, o=out, i=in_: e.dma_start(out=o, in_=i), s, 16))
        self._commit(tok, reads, writes)

    def barrier(self):
        for e in ENG:
            waits = []
            for s in range(len(self.sems)):
                v = self.semval[s]
                if v > 0 and self.seen[e].get(s, 0) < v:
                    self.seen[e][s] = v
                    waits.append((s, v))
            if waits:
                self.ops[e].append((waits, None, None, 0))
        self.lastw = {}
        self.readers = {}

    def _replay(self, e, eng):
        for waits, fn, s, inc in self.ops[e]:
            for ws, wv in waits:
                eng.wait_ge(self.sems[ws], wv)
            if fn is not None:
                ins = fn(eng)
                ins.then_inc(self.sems[s], inc)

    def emit(self):
        nc = self.nc
        with nc.Block() as block:
            @block.tensor
            def _(e):
                self._replay('pe', e)

            @block.scalar
            def _(e):
                self._replay('act', e)

            @block.vector
            def _(e):
                self._replay('dve', e)

            @block.gpsimd
            def _(e):
                self._replay('pool', e)

            @block.sync
            def _(e):
                self._replay('sp', e)
        return nc

    def mm(self, out, lhsT, rhs, start, stop, reads, writes):
        self.op('pe', lambda e: e.matmul(out, lhsT, rhs, start=start, stop=stop), reads, writes)

    def tr(self, out, in_, ident, reads, writes):
        self.op('pe', lambda e: e.transpose(out, in_, ident), reads, writes, cls='pe_tr')

    def act(self, out, in_, func, reads, writes, scale=None, bias=None, accum_out=None):
        kw = {}
        if scale is not None:
            kw['scale'] = scale
        if bias is not None:
            kw['bias'] = bias
        if accum_out is not None:
            kw['accum_out'] = accum_out
        self.op('act', lambda e: e.activation(out=out, in_=in_, func=func, **kw), reads, writes)

    def tt(self, eng, out, in0, in1, op, reads, writes):
        self.op(eng, lambda e: e.tensor_tensor(out=out, in0=in0, in1=in1, op=op), reads, writes)

    def ts(self, eng, out, in0, s1, op0, reads, writes, s2=None, op1=None):
        if op1 is None:
            self.op(eng, lambda e: e.tensor_scalar(out=out, in0=in0, scalar1=s1, scalar2=None, op0=op0), reads, writes)
        else:
            self.op(eng, lambda e: e.tensor_scalar(out=out, in0=in0, scalar1=s1, scalar2=s2, op0=op0, op1=op1), reads, writes)

    def stt(self, out, in0, scalar, in1, op0, op1, reads, writes):
        self.op('dve', lambda e: e.scalar_tensor_tensor(out=out, in0=in0, scalar=scalar, in1=in1, op0=op0, op1=op1),
                reads, writes)

    def cp(self, eng, out, in_, reads, writes):
        if eng == 'act':
            self.op('act', lambda e: e.copy(out=out, in_=in_), reads, writes)
        else:
            self.op(eng, lambda e: e.tensor_copy(out=out, in_=in_), reads, writes)

    def memset(self, eng, ap, val, writes):
        self.op(eng, lambda e: e.memset(ap, val), (), writes)


def build(debug=None):
    P = Prog()
    nc = P.nc

    def dram_in(name, shape, dt=F32):
        return nc.dram_tensor(name, list(shape), dt, kind="ExternalInput").ap()

    def dram_scr(name, shape, dt):
        return nc.dram_tensor(name, list(shape), dt, kind="Internal").ap()

    d_xT = dram_in("xT", [128, KC, U])
    d_cT = dram_in("cT", [128, KC, 2])
    d_wada = dram_in("w_ada", [L, 24, 128, KC * 128])
    d_bada = dram_in("b_adaT", [128, L * 24])
    d_ng = dram_in("norm_gT", [128, L * KC])
    d_fg = dram_in("final_gT", [128, KC])
    d_cw = dram_in("conv_wT", [128, L * KC * 5])
    d_cb = dram_in("conv_b", [L, 1, D])
    d_gb = dram_in("gate_bB", [128, L * 16])
    d_sink = dram_in("sink", [1, L * 8])
    d_hg = dram_in("head_gB", [L, 128, 512])
    d_wfm = dram_in("w_fm", [L, NFM, 128, KC * 128])
    d_wtm0 = dram_in("w_tm0", [L, 128, KC * 144])
    d_wtmh = dram_in("w_tmh", [L, 4, 128, KC * 384])
    d_woa = dram_in("w_out_a", [L, 8, 128, 4 * 128])
    d_wom = dram_in("w_out_m", [L, 8, 128, 4 * 128])
    d_ident = dram_in("ident", [128, 128])
    d_cos = dram_in("cosT", [128, T])
    d_sin = dram_in("sinT", [128, T])
    d_mle = dram_in("mask_le", [128, 128])
    d_mge = dram_in("mask_ge", [128, 128])
    d_nprev = dram_in("negprev", [128, 512])
    d_nnext = dram_in("negnext", [128, 512])
    d_sel = dram_in("sel", [128, 8])
    d_selden = dram_in("selden", [1, 256])
    d_out = nc.dram_tensor("outT", [128, KC, T], F32, kind="ExternalOutput").ap()

    s_q = dram_scr("s_q", [128, 4, U], BF16)
    s_k = dram_scr("s_k", [128, U], BF16)
    s_az = dram_scr("s_az", [128, 4, U], BF16)
    s_v = dram_scr("s_v", [128, NCH, 256], BF16)
    s_qm = dram_scr("s_qm", [4, 128, U], BF16)
    s_km = dram_scr("s_km", [4, 128, U], BF16)
    s_vv = dram_scr("s_vv", [4, 128, NCH, 129], BF16)
    s_to = dram_scr("s_to", [4, 128, NCH, 128], BF16)
    s_zg = dram_scr("s_zg", [4, 128, NCH, 128], BF16)

    dbg = {}
    if debug:
        for name, shape in debug.items():
            dbg[name] = nc.dram_tensor("dbg_" + name, list(shape), F32, kind="ExternalOutput").ap()

    xT = P.sb("xT", [128, KC, U], F32)
    cosT = P.sb("cosT", [128, T], F32)
    sinT = P.sb("sinT", [128, T], F32)
    identf = P.sb("identf", [128, 128], F32)
    identb = P.sb("identb", [128, 128], BF16)
    mle = P.sb("mle", [128, 128], F32)
    mge = P.sb("mge", [128, 128], F32)
    nprev = P.sb("nprev", [128, 512], BF16)
    nnext = P.sb("nnext", [128, 512], BF16)
    onesb = P.sb("onesb", [128, 128], BF16)
    onesf = P.sb("onesf", [128, 128], F32)
    onesrow = P.sb("onesrow", [1, 512], BF16)
    selden = P.sb("selden", [1, 256], BF16)
    sel = P.sb("sel", [128, 8], F32)
    cst = P.sb("cst", [128, 8], F32)
    badaT = P.sb("badaT", [128, L * 24], F32)
    ngs = P.sb("ngs", [128, L * KC], F32)
    fgs = P.sb("fgs", [128, KC], F32)
    cw = P.sb("cw", [128, L * KC * 5], F32)
    gbB = P.sb("gbB", [128, L * 16], F32)
    sinkrow = P.sb("sinkrow", [1, L * 8], F32)
    esf = P.sb("esf", [1, L * 8], F32)
    eshb = P.sb("eshb", [1, L * 8], BF16)
    eshf = P.sb("eshf", [1, L * 8], F32)
    eslf = P.sb("eslf", [1, L * 8], F32)
    esrow_hi = P.sb("esrow_hi", [1, 8 * 128], BF16)
    esrow_lo = P.sb("esrow_lo", [1, 8 * 128], BF16)
    cT = P.sb("cT", [128, KC * 2], F32)
    cth = P.sb("cth", [128, KC * 2], F32)
    sc2 = P.sb("sc2", [128, KC * 2], BF16)
    modT = P.sb("modT", [128, 48], F32)
    gs = P.sb("gs", [128, 16], F32)
    hgb = P.sb("hgb", [128, 512], F32)
    cbrow = P.sb("cbrow", [1, D], BF16)
    wo = [P.sb("wo%d" % i, [128, 512], BF16) for i in range(2)]
    gates = P.sb("gates", [128, NCH * 8], F32)
    nlfp = P.sb("nlfp", [128, NCH * 36], F32)
    apad = P.sb("apad", [128, NCH * 36], F32)
    fexp = P.sb("fexp", [128, NCH * 8], F32)
    nbs = P.sb("nbs", [128, NCH * 8], F32)
    ach = P.sb("ach", [128, NCH * 8], F32)
    Mb = P.sb("Mb", [128, NCH * 8], F32)
    c1b = P.sb("c1b", [128, NCH * 8], F32)
    ww = P.sb("ww", [128, NCH * 8], F32)
    Et = P.sb("Et", [128, NCH * 8], F32)
    tmpg = P.sb("tmpg", [128, NCH * 8], F32)
    mxT = P.sb("mxT", [128, NCH], F32)
    totT = P.sb("totT", [128, NCH], F32)
    Mt = P.sb("Mt", [128, NCH], F32)
    mt = P.sb("mt", [128, NCH], F32)
    dl = P.sb("dl", [128, NCH], F32)
    c1t = P.sb("c1t", [128, NCH], F32)
    RM = P.sb("RM", [128, NCH * 8], F32)
    RC = P.sb("RC", [128, NCH * 8], F32)
    persist_end = P.sb_off

    for kc in range(KC):
        P.dma('sp', xT[:, kc, :], d_xT[:, kc, :], (), [('xT', kc, g) for g in range(5)])
    for (sbt, dr, key) in [(cosT, d_cos, 'cos'), (sinT, d_sin, 'sin'), (identf, d_ident, 'identf'),
                           (mle, d_mle, 'mle'), (mge, d_mge, 'mge'), (sel, d_sel, 'sel'),
                           (badaT, d_bada, 'bada'), (ngs, d_ng, 'ngs'), (fgs, d_fg, 'fgs'), (cw, d_cw, 'cw'),
                           (gbB, d_gb, 'gbB'), (sinkrow, d_sink, 'sinkrow'), (cT, d_cT.rearrange("p a b -> p (a b)"), 'cT')]:
        P.dma('sp', sbt[:], dr, (), [key])
    for (sbt, dr, key) in [(identb, d_ident, 'identb'), (nprev, d_nprev, 'nprev'), (nnext, d_nnext, 'nnext'),
                           (selden, d_selden, 'selden')]:
        P.dma('pool', sbt[:], dr, (), [key])
    P.memset('pool', onesb[:], 1.0, ['onesb'])
    P.memset('pool', onesf[:], 1.0, ['onesf'])
    P.memset('pool', onesrow[:], 1.0, ['onesrow'])
    cvals = [0.0, 1.0, D * EPS, math.log(0.5), EPS * 4.0, math.log((128.0 ** -0.5) / 4.0), 0.0, 0.0]
    for i, v in enumerate(cvals):
        P.memset('pool', cst[:, i:i + 1], v, [('cst', i)])
    for t_ in (nlfp, apad, mxT, totT, Mt, mt, c1t, RM, RC):
        P.memset('pool', t_[:], 0.0, [t_.name])
    P.memset('pool', dl[:], -30000.0, ['dl'])
    K_NLFP, K_APAD, K_MXT, K_TOTT, K_MT, K_mt, K_C1T, K_RM, K_RC = [t_.name for t_ in (nlfp, apad, mxT, totT, Mt, mt, c1t, RM, RC)]
    P.ts('dve', ngs[:], ngs[:], 32.0, ALU.mult, ['ngs'], ['ngs'])
    P.ts('dve', fgs[:], fgs[:], 32.0, ALU.mult, ['fgs'], ['fgs'])
    P.act(cth[:], cT[:], AF.Tanh, ['cT'], ['cth'], scale=0.5)
    P.stt(cth[:], cth[:], 1.0, cT[:], ALU.add, ALU.mult, ['cth', 'cT'], ['cth'])
    P.ts('dve', sc2[:], cth[:], 0.5, ALU.mult, ['cth'], ['sc2'])
    P.act(esf[:], sinkrow[:], AF.Exp, ['sinkrow'], ['esf'])
    P.cp('dve', eshb[:], esf[:], ['esf'], ['eshb'])
    P.cp('dve', eshf[:], eshb[:], ['eshb'], ['eshf'])
    P.tt('dve', eslf[:], esf[:], eshf[:], ALU.subtract, ['esf', 'eshf'], ['eslf'])

    def bcast_last(ap2d, n):
        return ap2d.unsqueeze(2).to_broadcast([ap2d.shape[0], ap2d.shape[1], n])

    def xk(kc, g):
        return ('xT', kc, g)

    def dump(name, src_ap, key):
        if name in dbg:
            P.dma('sp', dbg[name], src_ap, [key], [('dbg', name)])

    for l in range(L):
        last = (l == L - 1)
        if l > 0:
            P.barrier()
            P.new_epoch()
        P.sb_off = persist_end
        ph1 = P.sb_off
        P.mark('L%d_ph0' % l)
        wa = [P.sb("wa%d" % i, [128, KC * 128], BF16) for i in range(3)]
        psm = P.bank()
        for cc in range(24):
            w = wa[cc % 3]
            wk = ('wa', cc % 3)
            P.dma('pool', w[:], d_wada[l, cc], (), [wk])
            for kc in range(KC):
                P.mm(P.ps[psm][:, 2 * cc:2 * cc + 2], w[:, kc * 128:(kc + 1) * 128], sc2[:, 2 * kc:2 * kc + 2],
                     kc == 0, kc == KC - 1, [wk, 'sc2'], [('ps', psm)])
        P.tt('dve', modT[:].rearrange("p (a b) -> p a b", b=2), P.ps[psm][:, 0:48].rearrange("p (a b) -> p a b", b=2),
             bcast_last(badaT[:, l * 24:(l + 1) * 24], 2), ALU.add, [('ps', psm), 'bada'], ['modT'])
        P.stt(gs[:].rearrange("p (a b) -> p a b", b=2), modT[:, 16:32].rearrange("p (a b) -> p a b", b=2), 1.0,
              bcast_last(ngs[:, l * KC:(l + 1) * KC], 2), ALU.add, ALU.mult, ['modT', 'ngs'], ['gs'])

        def shift_ap(kc, col):
            return modT[:, 2 * kc + col:2 * kc + col + 1]

        def gs_ap(kc, col):
            return gs[:, 2 * kc + col:2 * kc + col + 1]

        def gate_ap(kc, col):
            return modT[:, 32 + 2 * kc + col:32 + 2 * kc + col + 1]

        P.dma('sp', hgb[:], d_hg[l], (), ['hgb'])
        P.ts('dve', hgb[:], hgb[:], 0.5, ALU.mult, ['hgb'], ['hgb'])
        P.dma('pool', cbrow[:], d_cb[l], (), ['cbrow'])
        P.cp('dve', esrow_hi[:].rearrange("p (a b) -> p a b", b=128), bcast_last(eshb[:, l * 8:(l + 1) * 8], 128),
             ['eshb'], ['esrow_hi'])
        P.cp('dve', esrow_lo[:].rearrange("p (a b) -> p a b", b=128), bcast_last(eslf[:, l * 8:(l + 1) * 8], 128),
             ['eslf'], ['esrow_lo'])

        P.mark('L%d_ph1' % l)
        P.sb_off = ph1
        xnT = P.sb("xnT", [128, KC, U], BF16)
        sq = [P.sb("sq%d" % i, [128, 512], BF16) for i in range(3)]
        lnv = P.sb("lnv", [128, 512], F32)
        rstd = P.sb("rstd", [128, 512], F32)
        ut = [P.sb("ut%d" % i, [128, 512], F32) for i in range(3)]
        cnt = 0
        for g, (u0, n) in enumerate(GROUPS):
            col = 1 if g == 0 else 0
            pss = P.bank()
            for kc in range(KC):
                s_ = sq[cnt % 3]
                sk = ('sq', cnt % 3)
                src = xT[:, kc, u0:u0 + n]
                if cnt % 2 == 0:
                    P.act(s_[:, :n], src, AF.Square, [xk(kc, g)], [sk])
                else:
                    P.tt('pool', s_[:, :n], src, src, ALU.mult, [xk(kc, g)], [sk])
                cnt += 1
                P.mm(P.ps[pss][:, :n], onesb[:], s_[:, :n], kc == 0, kc == KC - 1, ['onesb', sk], [('ps', pss)])
            P.act(lnv[:, :n], P.ps[pss][:, :n], AF.Ln, [('ps', pss), ('cst', 2)], ['lnv'], bias=cst[:, 2:3])
            P.act(rstd[:, :n], lnv[:, :n], AF.Exp, ['lnv'], ['rstd'], scale=-0.5)
            for kc in range(KC):
                u_ = ut[kc % 3]
                uk = ('ut', kc % 3)
                P.tt('dve' if kc % 2 == 0 else 'pool', u_[:, :n], xT[:, kc, u0:u0 + n], rstd[:, :n], ALU.mult,
                     [xk(kc, g), 'rstd'], [uk])
                P.act(xnT[:, kc, u0:u0 + n], u_[:, :n], AF.Identity, [uk, 'gs', 'modT'], [('xn', g)],
                      scale=gs_ap(kc, col), bias=shift_ap(kc, col))

        P.mark('L%d_ph2' % l)
        wf = [P.sb("wf%d" % i, [128, KC * 128], BF16) for i in range(3)]
        wt = P.sb("wt", [128, KC * 384], BF16)
        raw = P.sb("raw", [128, U + 8], BF16)
        dg = P.sb("dg", [128, 40 * 128], BF16)
        tst = [P.sb("tst%d" % i, [128, 512], BF16) for i in range(2)]
        stg = sq
        t1 = [ut[0], ut[1]]
        t2 = [ut[2], lnv]
        T1K = [('ut', 0), ('ut', 1)]
        T2K = [('ut', 2), 'lnv']
        P.memset('pool', raw[:], 0.0, ['raw'])
        for i in range(40):
            P.ts('dve', dg[:, i * 128:(i + 1) * 128], identf[:], cw[:, l * 40 + i:l * 40 + i + 1], ALU.mult,
                 ['identf', 'cw'], [('dg', i)])
        sgc = [0]
        t1c = [0]

        def load_wf(ci):
            P.dma('pool', wf[ci % 3][:], d_wfm[l, ci], (), [('wf', ci % 3)])

        def proj_fm(ci, g):
            u0, n = GROUPS[g]
            b = P.bank()
            w = wf[ci % 3]
            for kc in range(KC):
                P.mm(P.ps[b][:, :n], w[:, kc * 128:(kc + 1) * 128], xnT[:, kc, u0:u0 + n], kc == 0, kc == KC - 1,
                     [('wf', ci % 3), ('xn', g)], [('ps', b)])
            return b

        def next_stg():
            i = sgc[0] % 3
            sgc[0] += 1
            return stg[i], ('sq', i)

        load_wf(0)
        load_wf(1)
        for pi in range(5):
            ci = 2 * pi
            if ci + 2 < NFM:
                load_wf(ci + 2)
            for g, (u0, n) in enumerate(GROUPS):
                b1 = proj_fm(ci, g)
                st_, sk_ = next_stg()
                if g == 0:
                    P.cp('act', st_[:, :n], P.ps[b1][:, :n], [('ps', b1)], [sk_])
                else:
                    b2 = proj_fm(ci + 1, g)
                    i = t1c[0] % 2
                    t1c[0] += 1
                    tt0 = u0 - LC
                    P.tt('dve', t1[i][:, :n], P.ps[b1][:, :n], cosT[:, tt0:tt0 + n], ALU.mult, [('ps', b1), 'cos'], [T1K[i]])
                    P.tt('dve', t2[i][:, :n], P.ps[b2][:, :n], sinT[:, tt0:tt0 + n], ALU.mult, [('ps', b2), 'sin'], [T2K[i]])
                    P.tt('pool', st_[:, :n], t1[i][:, :n], t2[i][:, :n], ALU.add, [T1K[i], T2K[i]], [sk_])
                dst = s_q[:, pi, u0:u0 + n] if pi < 4 else s_k[:, u0:u0 + n]
                dk = ('s_q', pi) if pi < 4 else 's_k'
                P.dma('sp', dst, st_[:, :n], [sk_], [dk])
            if ci + 3 < NFM:
                load_wf(ci + 3)
        P.mark('L%d_ph2az' % l)
        for j in range(4):
            ci = 10 + j
            if ci + 2 < NFM:
                load_wf(ci + 2)
            for g, (u0, n) in enumerate(GROUPS):
                b1 = proj_fm(ci, g)
                st_, sk_ = next_stg()
                i = t1c[0] % 2
                t1c[0] += 1
                P.act(t1[i][:, :n], P.ps[b1][:, :n], AF.Tanh, [('ps', b1)], [T1K[i]], scale=0.5)
                P.stt(st_[:, :n], t1[i][:, :n], 1.0, P.ps[b1][:, :n], ALU.add, ALU.mult, [T1K[i], ('ps', b1)], [sk_])
                P.dma('sp', s_az[:, j, u0:u0 + n], st_[:, :n], [sk_], [('s_az', j)])
        P.mark('L%d_ph2mqk' % l)
        for m in range(8):
            ci = 14 + m
            if ci + 2 < NFM:
                load_wf(ci + 2)
            for g, (u0, n) in enumerate(GROUPS):
                b1 = proj_fm(ci, g)
                r0 = u0 + 2 if g == 0 else u0 + 6
                P.cp('act', raw[:, r0:r0 + n], P.ps[b1][:, :n], [('ps', b1)], ['raw'])
            for g, (u0, n) in enumerate(GROUPS):
                r0 = u0 + 2 if g == 0 else u0 + 6
                b = P.bank()
                for jt in range(5):
                    P.mm(P.ps[b][:, :n], dg[:, (m * 5 + jt) * 128:(m * 5 + jt + 1) * 128],
                         raw[:, r0 + jt - 2:r0 + jt - 2 + n], jt == 0, False, [('dg', m * 5 + jt), 'raw'], [('ps', b)])
                P.mm(P.ps[b][:, :n], cbrow[0:1, m * 128:(m + 1) * 128], onesrow[0:1, :n], False, True,
                     ['cbrow', 'onesrow'], [('ps', b)])
                st_, sk_ = next_stg()
                i = t1c[0] % 2
                t1c[0] += 1
                P.act(t1[i][:, :n], P.ps[b][:, :n], AF.Tanh, [('ps', b)], [T1K[i]], scale=0.5)
                P.stt(st_[:, :n], t1[i][:, :n], 1.0, P.ps[b][:, :n], ALU.add, ALU.mult, [T1K[i], ('ps', b)], [sk_])
                if m < 4:
                    P.dma('sp', s_qm[m, :, u0:u0 + n], st_[:, :n], [sk_], [('s_qm', m)])
                else:
                    P.dma('sp', s_km[m - 4, :, u0:u0 + n], st_[:, :n], [sk_], [('s_km', m - 4)])
        P.mark('L%d_ph2tm0' % l)
        P.dma('pool', wt[:, 0:KC * 144], d_wtm0[l], (), ['wt'])
        tsc = [0]
        for c in range(NCH):
            b = P.bank()
            for kc in range(KC):
                P.mm(P.ps[b][:, 0:144], xnT[:, kc, c * 128:(c + 1) * 128], wt[:, kc * 144:(kc + 1) * 144],
                     kc == 0, kc == KC - 1, ['wt', ('xn', min(4, (c + 2) // 4))], [('ps', b)])
            i = tsc[0] % 2
            tsc[0] += 1
            vt = tst[i]
            vk = ('tst', i)
            P.memset('pool', vt[:, 0:256], 1.0, [vk])
            P.cp('act', vt[:, 0:64], P.ps[b][:, 0:64], [('ps', b)], [vk])
            P.cp('act', vt[:, 192:256], P.ps[b][:, 64:128], [('ps', b)], [vk])
            P.dma('sp', s_v[:, c, :], vt[:, 0:256], [vk], [('s_v', c)])
            P.tt('dve', gates[:, c * 8:(c + 1) * 8], P.ps[b][:, 128:136], gbB[:, l * 16:l * 16 + 8], ALU.add,
                 [('ps', b), 'gbB'], [('gates', c)])
            P.tt('dve', fexp[:, c * 8:(c + 1) * 8], P.ps[b][:, 136:144], gbB[:, l * 16 + 8:l * 16 + 16], ALU.add,
                 [('ps', b), 'gbB'], [('fexp', c)])

        P.mark('L%d_gate' % l)
        allg = [('gates', c) for c in range(NCH)]
        allf = [('fexp', c) for c in range(NCH)]
        P.act(fexp[:], fexp[:], AF.Exp, allf, ['fexp_e'], scale=-1.0)
        nl3 = nlfp[:].rearrange("p (c k) -> p c k", k=36)
        ap3 = apad[:].rearrange("p (c k) -> p c k", k=36)
        f3 = fexp[:].rearrange("p (c k) -> p c k", k=8)
        P.act(nl3[:, :, 0:4], f3[:, :, 0:4], AF.Ln, ['fexp_e', ('cst', 1)], [K_NLFP], bias=cst[:, 1:2])
        P.act(nl3[:, :, 32:36], f3[:, :, 4:8], AF.Ln, ['fexp_e', ('cst', 1)], [K_NLFP], bias=cst[:, 1:2])
        bnb = P.bank()
        for c in range(NCH):
            P.mm(P.ps[bnb][:, c * 8:c * 8 + 4], mle[:], nlfp[:, c * 36:c * 36 + 4], True, True, ['mle', K_NLFP], [('ps', bnb)])
            P.mm(P.ps[bnb][:, c * 8 + 4:c * 8 + 8], mge[:], nlfp[:, c * 36 + 32:c * 36 + 36], True, True,
                 ['mge', K_NLFP], [('ps', bnb)])
        P.cp('dve', nbs[:], P.ps[bnb][:, 0:NCH * 8], [('ps', bnb)], ['nbs'])
        P.tt('dve', ach[:], gates[:], nbs[:], ALU.add, allg + ['nbs'], ['ach'])
        a3 = ach[:].rearrange("p (c k) -> p c k", k=8)
        P.cp('dve', ap3[:, :, 0:4], a3[:, :, 0:4], ['ach'], [K_APAD])
        P.cp('dve', ap3[:, :, 32:36], a3[:, :, 4:8], ['ach'], [K_APAD])
        for c0 in range(0, NCH, 4):
            nc4 = min(4, NCH - c0)
            b = P.bank()
            for c in range(c0, c0 + nc4):
                P.mm(P.ps[b][0:36, (c - c0) * 128:(c - c0 + 1) * 128], apad[:, c * 36:(c + 1) * 36], identf[:], True, True,
                     [K_APAD, 'identf'], [('ps', b)])
            P.op('dve', lambda e, b=b, c0=c0, nc4=nc4: e.tensor_reduce(
                out=mxT[0:36, c0:c0 + nc4], in_=P.ps[b][0:36, 0:nc4 * 128].rearrange("p (c k) -> p c k", k=128),
                axis=AX.X, op=ALU.max), [('ps', b)], [K_MXT])
        btot = P.bank()
        for c in range(NCH):
            P.mm(P.ps[btot][0:36, c:c + 1], nlfp[:, c * 36:(c + 1) * 36], onesf[:, 0:1], True, True,
                 [K_NLFP, 'onesf'], [('ps', btot)])
        P.cp('dve', totT[0:36, :], P.ps[btot][0:36, 0:NCH], [('ps', btot)], [K_TOTT])
        for d in range(2):
            pp = slice(32 * d, 32 * d + 4)
            kM, km_, kd = ('Mt', d), ('mt', d), ('dl', d)
            for j, c in enumerate(ORDER[d]):
                if j == 0:
                    P.cp('dve', Mt[pp, c:c + 1], mxT[pp, c:c + 1], [K_MXT, K_MT], [kM])
                else:
                    cp_ = ORDER[d][j - 1]
                    P.tt('dve', Mt[pp, c:c + 1], mt[pp, cp_:cp_ + 1], mxT[pp, c:c + 1], ALU.max, [km_, K_MXT, K_MT], [kM])
                    P.tt('dve', dl[pp, c:c + 1], mt[pp, cp_:cp_ + 1], Mt[pp, c:c + 1], ALU.subtract, [km_, kM, 'dl'], [kd])
                P.tt('dve', mt[pp, c:c + 1], Mt[pp, c:c + 1], totT[pp, c:c + 1], ALU.subtract, [kM, K_TOTT, K_mt], [km_])
        P.act(c1t[0:36, :], dl[0:36, :], AF.Exp, [('dl', 0), ('dl', 1), 'dl', K_C1T], ['c1t_v'])
        sel3 = sel[0:36, :].unsqueeze(1).to_broadcast([36, NCH, 8])
        P.tt('dve', RM[0:36, :].rearrange("p (c k) -> p c k", k=8), bcast_last(Mt[0:36, :], 8), sel3, ALU.mult,
             [('Mt', 0), ('Mt', 1), 'sel', K_RM], ['RM_v'])
        P.tt('dve', RC[0:36, :].rearrange("p (c k) -> p c k", k=8), bcast_last(c1t[0:36, :], 8), sel3, ALU.mult,
             ['c1t_v', 'sel', K_RC], ['RC_v'])
        bM = P.bank()
        P.mm(P.ps[bM][:, 0:NCH * 8], onesf[0:36, :], RM[0:36, :], True, True, ['onesf', 'RM_v'], [('ps', bM)])
        P.mm(P.ps[bM][:, 256:256 + NCH * 8], onesf[0:36, :], RC[0:36, :], True, True, ['onesf', 'RC_v'], [('ps', bM)])
        P.cp('dve', Mb[:], P.ps[bM][:, 0:NCH * 8], [('ps', bM)], ['Mb'])
        P.cp('dve', c1b[:], P.ps[bM][:, 256:256 + NCH * 8], [('ps', bM)], ['c1b'])
        P.tt('dve', tmpg[:], ach[:], Mb[:], ALU.subtract, ['ach', 'Mb'], ['tmpg'])
        P.act(ww[:], tmpg[:], AF.Exp, ['tmpg', ('cst', 5)], ['ww'], bias=cst[:, 5:6])
        P.tt('dve', tmpg[:], nbs[:], Mb[:], ALU.subtract, ['nbs', 'Mb', 'ww'], ['tmpg'])
        P.act(Et[:], tmpg[:], AF.Exp, ['tmpg'], ['Et'])

        P.mark('L%d_ph2tmh' % l)
        for h in range(4):
            P.dma('pool', wt[:, 0:KC * 384], d_wtmh[l, h], (), ['wt'])
            for c in range(NCH):
                b = P.bank()
                for kc in range(KC):
                    P.mm(P.ps[b][:, 0:384], xnT[:, kc, c * 128:(c + 1) * 128], wt[:, kc * 384:(kc + 1) * 384],
                         kc == 0, kc == KC - 1, ['wt', ('xn', min(4, (c + 2) // 4))], [('ps', b)])
                i = tsc[0] % 2
                tsc[0] += 1
                vt = tst[i]
                vk = ('tst', i)
                P.memset('pool', vt[:, 128:129], 1.0, [vk])
                P.cp('act', vt[:, 0:128], P.ps[b][:, 0:128], [('ps', b)], [vk])
                P.act(vt[:, 129:257], P.ps[b][:, 128:256], AF.Tanh, [('ps', b)], [vk], scale=0.5)
                j_ = t1c[0] % 2
                t1c[0] += 1
                P.act(t1[j_][:, 0:128], P.ps[b][:, 256:384], AF.Tanh, [('ps', b)], [T1K[j_]], scale=0.5)
                P.stt(vt[:, 257:385], t1[j_][:, 0:128], 1.0, P.ps[b][:, 256:384], ALU.add, ALU.mult,
                      [T1K[j_], ('ps', b)], [vk])
                P.dma('sp', s_vv[h, :, c, :], vt[:, 0:129], [vk], [('s_vv', h)])
                P.dma('sp', s_to[h, :, c, :], vt[:, 129:257], [vk], [('s_to', h)])
                P.dma('sp', s_zg[h, :, c, :], vt[:, 257:385], [vk], [('s_zg', h)])

        P.mark('L%d_ph3' % l)
        P.barrier()
        P.sb_off = ph1
        qT = P.sb("qT", [128, 4, U], BF16)
        kT = P.sb("kT", [128, U], BF16)
        Va = P.sb("Va", [128, NCH, 256], BF16)
        azT = P.sb("azT", [128, 4, U], BF16)
        PT = [P.sb("PT%d" % i, [128, 512], BF16) for i in range(8)]
        lnd = [P.sb("lnd%d" % i, [128, 512], F32) for i in range(2)]
        rdn = [P.sb("rdn%d" % i, [128, 512], F32) for i in range(2)]
        tnm = [P.sb("tnm%d" % i, [128, 512], F32) for i in range(2)]
        for j in range(4):
            P.dma('sp', qT[:, j, :], s_q[:, j, :], [('s_q', j)], ['qT'])
            P.dma('sp', azT[:, j, :], s_az[:, j, :], [('s_az', j)], [('azT', j, g) for g in range(5)])
        P.dma('sp', kT[:], s_k[:], ['s_k'], ['kT'])
        P.dma('sp', Va[:], s_v[:], [('s_v', c) for c in range(NCH)], ['Va'])
        lc = [0]
        qblocks = list(range(2, NCH)) + ([] if last else [0, 1])
        tasks = []
        units = []
        for qc in qblocks:
            if qc >= 2:
                n_ = qc - 2
                ktiles = []
                if n_ > 0:
                    ktiles.append((qc - 1, nprev))
                ktiles.append((qc, None))
                if n_ < 15:
                    ktiles.append((qc + 1, nnext))
                ktiles += [(0, None), (1, None)]
            else:
                ktiles = [(0, None), (1, None)]
            ub = len(units)
            units.append((qc, 0))
            units.append((qc, 1))
            for ti, (kc_, msk) in enumerate(ktiles):
                for g in range(2):
                    tasks.append((ub + g, kc_, msk, ti == 0, ti == len(ktiles) - 1))
        NT = len(tasks)
        LA = 3
        SBK = [0, 1, 2, 3]
        POK = [4, 5, 6, 7]
        for k in range(NT + LA):
            if k < NT:
                u_, kc_, msk, first, lastt = tasks[k]
                qc, g = units[u_]
                qu0 = qc * 128
                pr = slice(64 * g, 64 * g + 64)
                b = SBK[k % 4]
                P.mm(P.ps[b][:].rearrange("p (j q) -> p j q", q=128), kT[pr, kc_ * 128:(kc_ + 1) * 128],
                     qT[pr, :, qu0:qu0 + 128], True, msk is None, ['kT', 'qT'], [('ps', b)])
                if msk is not None:
                    P.mm(P.ps[b][:], identb[:], msk[:], False, True, ['identb', 'nprev', 'nnext'], [('ps', b)])
                P.act(PT[k % 8][:], P.ps[b][:], AF.Exp, [('ps', b)], [('PT', k % 8)], scale=0.125)
            k2 = k - LA
            if k2 >= 0:
                u_, kc_, msk, first, lastt = tasks[k2]
                qc, g = units[u_]
                qu0 = qc * 128
                qg = min(4, (qc + 2) // 4)
                pr = slice(64 * g, 64 * g + 64)
                dr = slice(64 - 64 * g, 128 - 64 * g)
                bo = POK[u_ % 4]
                P.mm(P.ps[bo][:], Va[:, kc_, g * 128:(g + 1) * 128], PT[k2 % 8][:], first, False, ['Va', ('PT', k2 % 8)], [('ps', bo)])
                if lastt:
                    P.mm(P.ps[bo][:], selden[0:1, g * 128:(g + 1) * 128], esrow_hi[0:1, g * 512:(g + 1) * 512], False, False,
                         ['selden', 'esrow_hi'], [('ps', bo)])
                    P.mm(P.ps[bo][:], selden[0:1, g * 128:(g + 1) * 128], esrow_lo[0:1, g * 512:(g + 1) * 512], False, True,
                         ['selden', 'esrow_lo'], [('ps', bo)])
                    i2 = lc[0] % 2
                    lc[0] += 1
                    P.act(lnd[i2][pr, :], P.ps[bo][dr, :], AF.Ln, [('ps', bo)], [('lnd', i2)])
                    P.act(rdn[i2][pr, :], lnd[i2][pr, :], AF.Exp, [('lnd', i2), ('cst', 3)], [('rdn', i2)], scale=-1.0, bias=cst[pr, 3:4])
                    P.tt('dve', tnm[i2][pr, :], P.ps[bo][pr, :], rdn[i2][pr, :], ALU.mult, [('ps', bo), ('rdn', i2)], [('tnm', i2)])
                    P.tt('pool', azT[pr, :, qu0:qu0 + 128], tnm[i2][pr, :].rearrange("p (j q) -> p j q", q=128),
                         azT[pr, :, qu0:qu0 + 128], ALU.mult, [('tnm', i2)] + [('azT', j, qg) for j in range(4)],
                         [('azT', j, qg) for j in range(4)])
        P.mark('L%d_ph3o' % l)
        for dc in range(8):
            w = wo[dc % 2]
            P.dma('pool', w[:], d_woa[l, dc], (), [('wo', dc % 2)])
            for g, (u0, n) in enumerate(GROUPS):
                if last and g == 0:
                    continue
                col = 1 if g == 0 else 0
                b = P.bank()
                for j in range(4):
                    P.mm(P.ps[b][:, :n], w[:, j * 128:(j + 1) * 128], azT[:, j, u0:u0 + n], j == 0, j == 3,
                         [('wo', dc % 2), ('azT', j, g)], [('ps', b)])
                P.stt(xT[:, dc, u0:u0 + n], P.ps[b][:, :n], gate_ap(dc, col), xT[:, dc, u0:u0 + n], ALU.mult, ALU.add,
                      [('ps', b), 'modT', xk(dc, g)], [xk(dc, g)])

        P.mark('L%d_ph4' % l)
        P.barrier()
        P.sb_off = ph1
        mT = P.sb("mT", [128, 4, U], BF16)
        qm = [P.sb("qm%d" % i, [128, U], BF16) for i in range(2)]
        km = [P.sb("km%d" % i, [128, U], BF16) for i in range(2)]
        vv = [P.sb("vv%d" % i, [128, NCH, 129], BF16) for i in range(2)]
        tho = [P.sb("tho%d" % i, [128, NCH, 128], BF16) for i in range(2)]
        zg = [P.sb("zg%d" % i, [128, NCH, 128], BF16) for i in range(2)]
        hacc1 = P.sb("hacc", [128, NCH, 128], F32)
        hacc = [hacc1, hacc1]
        Cst = [[P.sb("Cst%d%d" % (a, b_), [128, 129], F32) for b_ in range(2)] for a in range(2)]
        Crb = [[P.sb("Crb%d%d" % (a, b_), [128, 129], BF16) for b_ in range(2)] for a in range(2)]
        PTm = [P.sb("PTm%d" % i, [128, 128], BF16) for i in range(4)]
        kwm = [P.sb("kwm%d" % i, [128, 128], BF16) for i in range(4)]
        dn = [P.sb("dn%d" % i, [128, 2], F32) for i in range(8)]
        hgt = [P.sb("hgt%d" % i, [128, 128], F32) for i in range(4)]
        zgh = [P.sb("zgh%d" % i, [128, 128], F32) for i in range(4)]
        sqj = [P.sb("sqj%d" % i, [128, 128], F32) for i in range(2)]
        ssr = [P.sb("ssr%d" % i, [128, 4], F32) for i in range(8)]
        mtk = [P.sb("mtk%d" % i, [128, 128], BF16) for i in range(4)]
        ABK = [0]
        BBK = [1]
        OUK = [2, 3, 4, 5]
        UBK = 6
        TBK = 7

        def cgrp(c):
            return min(4, (c + 2) // 4)

        def load_head(h, i):
            ks = [('qm', i, g) for g in range(5)]
            P.dma('sp', qm[i][:], s_qm[h], [('s_qm', h)], [('qm', i, g) for g in range(5)])
            P.dma('sp', km[i][:], s_km[h], [('s_km', h)], [('km', i, g) for g in range(5)])
            P.dma('sp', vv[i][:], s_vv[h], [('s_vv', h)], [('vv', i, g) for g in range(5)])
            P.dma('sp', tho[i][:], s_to[h], [('s_to', h)], [('tho', i, g) for g in range(5)])
            P.dma('sp', zg[i][:], s_zg[h], [('s_zg', h)], [('zg', i, g) for g in range(5)])

        J1 = {c: ORDER[1].index(c) for c in range(NCH)}
        pcnt = [0]
        for pair in range(1):
            load_head(0, 0)
            steps = [(h_, j, d) for h_ in range(4) for j in range(NCH) for d in range(2)]
            NS = len(steps)
            info = {}
            for s_, (h_, j, d) in enumerate(steps):
                c = ORDER[d][j]
                h = h_
                hs = h_ % 2
                other = J1[c] if d == 0 else c
                info[s_] = dict(hs=hs, j=j, d=d, c=c, h=h, col=d * 4 + h, g=cgrp(c), second=(other < j),
                                need_h=not (last and c < 2), csl=slice(c * 128, (c + 1) * 128))

            def S1(s_):
                I = info[s_]
                hs, c, g, col, csl = I['hs'], I['c'], I['g'], I['col'], I['csl']
                ab = ABK[0]
                r = s_ % 4
                wcol = ww[:, c * 8 + col:c * 8 + col + 1]
                P.mm(P.ps[ab][:, 0:128], km[hs][:, csl], qm[hs][:, csl], True, True, [('km', hs, g), ('qm', hs, g)], [('ps', ab)])
                bb = BBK[0]
                psb16 = P.ps[bb][:].bitcast(BF16)
                P.tr(psb16[:, 0:128], km[hs][:, csl], identb[:], [('km', hs, g), 'identb'], [('ps', bb)])
                P.stt(PTm[r][:], P.ps[ab][:, 0:128], wcol, (mle if I['d'] == 0 else mge)[:], ALU.mult, ALU.mult,
                      [('ps', ab), 'ww', 'mle', 'mge'], [('PTm', r)])
                P.act(kwm[r][:], psb16[:, 0:128], AF.Identity, [('ps', bb), 'ww'], [('kwm', r)], scale=wcol)

            def S2(s_):
                I = info[s_]
                hs, c, g, col, csl, j, d = I['hs'], I['c'], I['g'], I['col'], I['csl'], I['j'], I['d']
                ou = OUK[s_ % 4]
                r = s_ % 4
                if j > 0:
                    P.mm(P.ps[ou][:, 0:129], qm[hs][:, csl], Crb[hs][d][:], True, False, [('qm', hs, g), ('Crb', hs, d)], [('ps', ou)])
                P.mm(P.ps[ou][:, 0:129], PTm[r][:], vv[hs][:, c, :], j == 0, True, [('PTm', r), ('vv', hs, g)], [('ps', ou)])
                P.mm(P.ps[UBK][:, 0:129], kwm[r][:], vv[hs][:, c, :], True, True, [('kwm', r), ('vv', hs, g)], [('ps', UBK)])

            def S2a(s_):
                I = info[s_]
                hs, c, col, j, d = I['hs'], I['c'], I['col'], I['j'], I['d']
                if j > 0:
                    P.ts('pool', Crb[hs][d][:], Cst[hs][d][:], c1b[:, c * 8 + col:c * 8 + col + 1], ALU.mult,
                         [('Cst', hs, d), 'c1b'], [('Crb', hs, d)])

            def S3a(s_):
                I = info[s_]
                hs, c, col, j, d = I['hs'], I['c'], I['col'], I['j'], I['d']
                ou = OUK[s_ % 4]
                q = s_ % 8
                P.tt('dve', dn[q][:, 1:2], P.ps[ou][:, 128:129], Et[:, c * 8 + col:c * 8 + col + 1], ALU.max,
                     [('ps', ou), 'Et'], [('dn', q)])
                if j == 0:
                    P.cp('dve', Cst[hs][d][:], P.ps[UBK][:, 0:129], [('ps', UBK)], [('Cst', hs, d)])
                else:
                    P.stt(Cst[hs][d][:], Cst[hs][d][:], c1b[:, c * 8 + col:c * 8 + col + 1], P.ps[UBK][:, 0:129], ALU.mult, ALU.add,
                          [('Cst', hs, d), 'c1b', ('ps', UBK)], [('Cst', hs, d)])

            def S3b(s_):
                ou = OUK[s_ % 4]
                q = s_ % 8
                P.stt(dn[q][:, 0:1], P.ps[ou][:, 128:129], -1.0, dn[q][:, 1:2], ALU.mult, ALU.max,
                      [('ps', ou), ('dn', q)], [('dn', q)])

            def S3c(s_):
                q = s_ % 8
                P.op('dve', lambda e, q=q, dn=dn: e.reciprocal(out=dn[q][:, 1:2], in_=dn[q][:, 0:1]), [('dn', q)], [('dn', q)])

            def S3d(s_):
                I = info[s_]
                hs, c = I['hs'], I['c']
                ou = OUK[s_ % 4]
                q = s_ % 8
                if not I['need_h']:
                    return
                if not I['second']:
                    P.act(hacc[hs][:, c, :], P.ps[ou][:, 0:128], AF.Identity, [('ps', ou), ('dn', q)], [('hacc', c)],
                          scale=dn[q][:, 1:2])
                else:
                    P.stt(hacc[hs][:, c, :], P.ps[ou][:, 0:128], dn[q][:, 1:2], hacc[hs][:, c, :], ALU.mult, ALU.add,
                          [('ps', ou), ('dn', q), ('hacc', c)], [('hacc', c)])
                    I['p'] = pcnt[0]
                    pcnt[0] += 1

            def post_ok(s_):
                return 'p' in info[s_]

            def S3e(s_):
                if not post_ok(s_):
                    return
                I = info[s_]
                hs, c, g, h = I['hs'], I['c'], I['g'], I['h']
                p = I['p']
                P.stt(hgt[p % 4][:], tho[hs][:, c, :], 1.0, hacc[hs][:, c, :], ALU.add, ALU.mult,
                      [('tho', hs, g), ('hacc', c)], [('hgt', p % 4)])
                P.tt('pool', zgh[p % 4][:], zg[hs][:, c, :], hgb[:, h * 128:(h + 1) * 128], ALU.mult,
                     [('zg', hs, g), 'hgb'], [('zgh', p % 4)])

            def S3f(s_):
                if not post_ok(s_):
                    return
                p = info[s_]['p']
                P.act(sqj[p % 2][:], hgt[p % 4][:], AF.Square, [('hgt', p % 4)], [('sqj', p % 2), ('ssr', p % 8)],
                      accum_out=ssr[p % 8][:, 0:1])

            def S3g(s_):
                if not post_ok(s_):
                    return
                p = info[s_]['p']
                P.act(ssr[p % 8][:, 1:2], ssr[p % 8][:, 0:1], AF.Ln, [('ssr', p % 8), ('cst', 4)], [('ssr', p % 8)],
                      scale=1.0 / 128.0, bias=cst[:, 4:5])

            def S3h(s_):
                if not post_ok(s_):
                    return
                p = info[s_]['p']
                P.act(ssr[p % 8][:, 2:3], ssr[p % 8][:, 1:2], AF.Exp, [('ssr', p % 8)], [('ssr', p % 8)], scale=-0.5)

            def S3i(s_):
                if not post_ok(s_):
                    return
                p = info[s_]['p']
                P.stt(mtk[p % 4][:], hgt[p % 4][:], ssr[p % 8][:, 2:3], zgh[p % 4][:], ALU.mult, ALU.mult,
                      [('hgt', p % 4), ('ssr', p % 8), ('zgh', p % 4)], [('mtk', p % 4)])

            def S3j(s_):
                if not post_ok(s_):
                    return
                p = info[s_]['p']
                pst16 = P.ps[TBK][:].bitcast(BF16)
                P.tr(pst16[:, 0:128], mtk[p % 4][:], identb[:], [('mtk', p % 4), 'identb'], [('ps', TBK)])

            def S3k(s_):
                if not post_ok(s_):
                    return
                I = info[s_]
                pst16 = P.ps[TBK][:].bitcast(BF16)
                P.cp('act', mT[:, I['h'], I['csl']], pst16[:, 0:128], [('ps', TBK)], [('mT', I['h'], I['g'])])

            def S3all(s_):
                for f_ in (S3a, S3b, S3c, S3d):
                    f_(s_)

            def Spost(s_):
                for f_ in (S3e, S3f, S3g, S3h, S3i, S3j, S3k):
                    f_(s_)

            stages = [S1, S2, S3all, Spost]
            def S3bc(s_):
                S3b(s_)
                S3c(s_)

            sched = [(S2a, 1), (S1, 0), (S3a, 2), (S3bc, 3), (S3d, 4), (S2, 1), (Spost, 5)]
            stages = [None] * 7
            for it in range(NS + len(stages)):
                if it >= len(stages) and (it - len(stages)) % (2 * NCH) == 0:
                    hn = (it - len(stages)) // (2 * NCH) + 1
                    if hn < 4:
                        load_head(hn, hn % 2)
                for fn_, k_ in sched:
                    s_ = it - k_
                    if 0 <= s_ < NS:
                        fn_(s_)
        P.mark('L%d_ph5' % l)
        for dc in range(8):
            w = wo[dc % 2]
            P.dma('pool', w[:], d_wom[l, dc], (), [('wo', dc % 2)])
            for g, (u0, n) in enumerate(GROUPS):
                if last and g == 0:
                    continue
                col = 1 if g == 0 else 0
                b = P.bank()
                for hh in range(4):
                    P.mm(P.ps[b][:, :n], w[:, hh * 128:(hh + 1) * 128], mT[:, hh, u0:u0 + n], hh == 0, hh == 3,
                         [('wo', dc % 2), ('mT', hh, g)], [('ps', b)])
                P.stt(xT[:, dc, u0:u0 + n], P.ps[b][:, :n], gate_ap(dc, col), xT[:, dc, u0:u0 + n], ALU.mult, ALU.add,
                      [('ps', b), 'modT', xk(dc, g)], [xk(dc, g)])

    P.mark('final')
    P.barrier()
    P.new_epoch()
    P.sb_off = persist_end
    sq = [P.sb("fsq%d" % i, [128, 512], BF16) for i in range(3)]
    lnv = P.sb("flnv", [128, 512], F32)
    rstd = P.sb("frstd", [128, 512], F32)
    ot = [P.sb("fot%d" % i, [128, 512], F32) for i in range(3)]
    cnt = 0
    oc = 0
    for g in range(1, 5):
        u0, n = GROUPS[g]
        pss = P.bank()
        for kc in range(KC):
            s_ = sq[cnt % 3]
            sk = ('fsq', cnt % 3)
            src = xT[:, kc, u0:u0 + n]
            if cnt % 2 == 0:
                P.act(s_[:, :n], src, AF.Square, [xk(kc, g)], [sk])
            else:
                P.tt('pool', s_[:, :n], src, src, ALU.mult, [xk(kc, g)], [sk])
            cnt += 1
            P.mm(P.ps[pss][:, :n], onesb[:], s_[:, :n], kc == 0, kc == KC - 1, ['onesb', sk], [('ps', pss)])
        P.act(lnv[:, :n], P.ps[pss][:, :n], AF.Ln, [('ps', pss), ('cst', 2)], ['flnv'], bias=cst[:, 2:3])
        P.act(rstd[:, :n], lnv[:, :n], AF.Exp, ['flnv'], ['frstd'], scale=-0.5)
        for kc in range(KC):
            o_ = ot[oc % 3]
            ok = ('fot', oc % 3)
            oc += 1
            P.stt(o_[:, :n], xT[:, kc, u0:u0 + n], fgs[:, kc:kc + 1], rstd[:, :n], ALU.mult, ALU.mult,
                  [xk(kc, g), 'fgs', 'frstd'], [ok])
            P.dma('sp', d_out[:, kc, u0 - LC:u0 - LC + n], o_[:, :n], [ok], [('out', kc, g)])
    P.barrier()
    P.emit()
    return nc, P


_CONST_CACHE = {}


def _consts():
    if _CONST_CACHE:
        return _CONST_CACHE
    f32 = np.float32
    ident = np.eye(128, dtype=f32)
    s = np.arange(128)[:, None]
    t = np.arange(128)[None, :]
    mask_le = (s <= t).astype(f32)
    mask_ge = (s >= t).astype(f32)
    NEG = -30000.0
    negprev = np.where(t > s, NEG, 0.0).astype(f32)
    negnext = np.where(s > t, NEG, 0.0).astype(f32)
    negprev = np.tile(negprev, (1, 4))
    negnext = np.tile(negnext, (1, 4))
    sel = np.zeros((128, 8), f32)
    for i in range(4):
        sel[i, i] = 1.0
        sel[32 + i, 4 + i] = 1.0
    selden = np.zeros((1, 256), f32)
    selden[0, 64:128] = 1.0
    selden[0, 128:192] = 1.0
    half = 32
    inv = (10000.0 ** (-np.arange(0, half, 2, dtype=f32) / half)).astype(f32)
    tt_ = np.arange(T)
    row = (tt_ // 64).astype(f32)
    colp = (tt_ % 64).astype(f32)
    cosT = np.zeros((128, T), f32)
    sinT = np.zeros((128, T), f32)
    for p in range(128):
        f = p % 64
        pos = row if f < 32 else colp
        ang = (pos * inv[f % 16]).astype(f32)
        cosT[p] = np.cos(ang).astype(f32)
        sgn = -1.0 if (f % 32) < 16 else 1.0
        sinT[p] = (sgn * np.sin(ang)).astype(f32)
    _CONST_CACHE.update(dict(ident=ident, mask_le=mask_le, mask_ge=mask_ge, negprev=negprev, negnext=negnext,
                             sel=sel, selden=selden, cosT=cosT, sinT=sinT))
    return _CONST_CACHE


def _kchunk(w):
    n = w.shape[1]
    return np.ascontiguousarray(w.reshape(KC, 128, n).transpose(1, 0, 2).reshape(128, KC * n))


def _prep_shared(w_ada, b_ada, norm_g, w_in, conv_w, conv_b, gate_b, sink, head_g, w_out, final_g):
    f32 = np.float32
    a64 = np.arange(64)
    sw = np.where((a64 % 32) < 16, a64 + 16, a64 - 16)
    fm_cols = []
    for j in range(4):
        fm_cols.append(np.concatenate([j * 64 + a64, (4 + j) * 64 + a64]))
        fm_cols.append(np.concatenate([j * 64 + sw, (4 + j) * 64 + sw]))
    fm_cols.append(512 + np.arange(128))
    fm_cols.append(512 + np.concatenate([sw, 64 + sw]))
    for j in range(4):
        fm_cols.append(768 + np.concatenate([j * 64 + a64, (4 + j) * 64 + a64]))
    for h in range(4):
        fm_cols.append(1280 + h * 128 + np.arange(128))
    for h in range(4):
        fm_cols.append(1792 + h * 128 + np.arange(128))
    assert len(fm_cols) == NFM
    tm0 = np.concatenate([640 + np.arange(128), 3840 + np.arange(16)])
    tmh = [np.concatenate([2304 + h * 128 + np.arange(128), 2816 + h * 128 + np.arange(128),
                           3328 + h * 128 + np.arange(128)]) for h in range(4)]
    w_fm = np.empty((L, NFM, 128, KC * 128), f32)
    w_tm0 = np.empty((L, 128, KC * 144), f32)
    w_tmh = np.empty((L, 4, 128, KC * 384), f32)
    w_ada_l = np.empty((L, 24, 128, KC * 128), f32)
    w_out_a = np.empty((L, 8, 128, 512), f32)
    w_out_m = np.empty((L, 8, 128, 512), f32)
    for l in range(L):
        for ci, cols in enumerate(fm_cols):
            w_fm[l, ci] = _kchunk(w_in[l][:, cols])
        w_tm0[l] = _kchunk(w_in[l][:, tm0])
        for h in range(4):
            w_tmh[l, h] = _kchunk(w_in[l][:, tmh[h]])
        for cc in range(24):
            w_ada_l[l, cc] = _kchunk(w_ada[l][:, cc * 128:(cc + 1) * 128])
        for dc in range(8):
            for j in range(4):
                rows = np.concatenate([j * 64 + a64, (4 + j) * 64 + a64])
                w_out_a[l, dc, :, j * 128:(j + 1) * 128] = w_out[l][rows, dc * 128:(dc + 1) * 128]
                rows_m = 512 + j * 128 + np.arange(128)
                w_out_m[l, dc, :, j * 128:(j + 1) * 128] = w_out[l][rows_m, dc * 128:(dc + 1) * 128]
    b_adaT = np.ascontiguousarray(b_ada.reshape(L, 24, 128).transpose(2, 0, 1).reshape(128, L * 24))
    norm_gT = np.ascontiguousarray(norm_g.reshape(L, KC, 128).transpose(2, 0, 1).reshape(128, L * KC))
    final_gT = np.ascontiguousarray(final_g.reshape(KC, 128).T)
    conv_wT = np.ascontiguousarray(conv_w.reshape(L, 5, KC, 128).transpose(3, 0, 2, 1).reshape(128, L * KC * 5))
    conv_b_r = np.ascontiguousarray(conv_b.reshape(L, 1, D))
    gate_bB = np.ascontiguousarray(np.broadcast_to(gate_b.reshape(1, L * 16), (128, L * 16)))
    sink_r = np.ascontiguousarray(sink.reshape(1, L * 8))
    head_gB = np.ascontiguousarray(np.broadcast_to(head_g.reshape(L, 1, 512), (L, 128, 512)))
    d = dict(w_ada=w_ada_l, b_adaT=b_adaT, norm_gT=norm_gT, final_gT=final_gT, conv_wT=conv_wT, conv_b=conv_b_r,
             gate_bB=gate_bB, sink=sink_r, head_gB=head_gB, w_fm=w_fm, w_tm0=w_tm0, w_tmh=w_tmh,
             w_out_a=w_out_a, w_out_m=w_out_m)
    d.update(_consts())
    return {k: np.ascontiguousarray(v, dtype=np.float32) for k, v in d.items()}


def _prep_core(x_b, ctx_b, c_b, c_ctx):
    xa = np.concatenate([ctx_b, x_b], axis=0)
    xT = np.ascontiguousarray(xa.T.reshape(KC, 128, U).transpose(1, 0, 2))
    cc = np.stack([c_b, c_ctx], axis=-1)
    cT = np.ascontiguousarray(cc.reshape(KC, 128, 2).transpose(1, 0, 2))
    return dict(xT=xT.astype(np.float32), cT=cT.astype(np.float32))


def kernel(x, c, ctx, c_ctx, w_ada, b_ada, norm_g, w_in, conv_w, conv_b, gate_b, sink, head_g, w_out, final_g,
           _debug=None, _cores=8):
    arrs = [np.asarray(a, dtype=np.float32) for a in
            (x, c, ctx, c_ctx, w_ada, b_ada, norm_g, w_in, conv_w, conv_b, gate_b, sink, head_g, w_out, final_g)]
    x, c, ctx, c_ctx, w_ada, b_ada, norm_g, w_in, conv_w, conv_b, gate_b, sink, head_g, w_out, final_g = arrs
    shared = _prep_shared(w_ada, b_ada, norm_g, w_in, conv_w, conv_b, gate_b, sink, head_g, w_out, final_g)
    nc, P = build(_debug)
    in_maps = []
    for b in range(_cores):
        m = dict(shared)
        m.update(_prep_core(x[b], ctx[b], c[b], c_ctx))
        in_maps.append(m)
    res = run_bass_kernel_spmd(nc, in_maps, core_ids=list(range(_cores)))
    outs = []
    for b in range(_cores):
        oT = res.results[b]["outT"]
        outs.append(np.ascontiguousarray(oT.transpose(1, 0, 2).reshape(D, T).T))
    out = np.stack(outs, axis=0).astype(np.float32)
    if _debug:
        return out, res
    return out
```

```python
import math
import numpy as np
import concourse.bass as bass
import concourse.mybir as mybir
from concourse.bass_utils import run_bass_kernel_spmd

F32 = mybir.dt.float32
BF16 = mybir.dt.bfloat16
AF = mybir.ActivationFunctionType
ALU = mybir.AluOpType
AX = mybir.AxisListType

L = 4
D = 1024
T = 2048
LC = 256
U = T + LC
NCH = U // 128
KC = 8
EPS = 1e-6
GROUPS = [(0, 256), (256, 512), (768, 512), (1280, 512), (1792, 512)]
ORDER = [list(range(NCH)), [1, 0] + list(range(NCH - 1, 1, -1))]
NFM = 22

ENG = ['pe', 'act', 'dve', 'pool', 'sp']
SAME_ENGINE_SYNC = {'act', 'dve', 'pool'}


def _size(dt):
    return 2 if dt == BF16 else 4


class Prog:
    def __init__(self):
        self.nc = bass.Bass("TRN2", target_bir_lowering=False)
        self.ops = {e: [] for e in ENG}
        self.seen = {e: {} for e in ENG}
        self.lastw = {}
        self.readers = {}
        self.sb_off = 16512
        self.sb_max = 0
        self.sems = []
        self.semval = []
        self.esem = {}
        for e in ENG:
            self.esem[e] = self._newsem("e_" + e)
        self.esem['pe_tr'] = self._newsem("e_pe_tr")
        self.dq = {'sp': [self._newsem("dsp%d" % i) for i in range(10)],
                   'pool': [self._newsem("dpl%d" % i) for i in range(6)],
                   'act': [self._newsem("dac%d" % i) for i in range(2)]}
        self.drr = {'sp': 0, 'pool': 0, 'act': 0}
        self.ps = [self.nc.alloc_psum_tensor("psb%d" % i, [128, 512], F32) for i in range(8)]
        self.psrr = 0
        self.nm = 0
        self.marks = []

    def _newsem(self, name):
        h = self.nc.alloc_semaphore(name=name)
        self.sems.append(h)
        self.semval.append(0)
        return len(self.sems) - 1

    def mark(self, label):
        self.marks.append((label, {e: len(self.ops[e]) for e in ENG}))

    def new_epoch(self):
        for e in ENG:
            self.esem[e] = self._newsem("e%d_%s" % (len(self.sems), e))
        self.esem['pe_tr'] = self._newsem("e%d_pe_tr" % len(self.sems))

    def sb(self, name, shape, dt):
        nbytes = int(np.prod(shape[1:])) * _size(dt)
        off = (self.sb_off + 31) // 32 * 32
        self.sb_off = off + nbytes
        self.sb_max = max(self.sb_max, self.sb_off)
        assert self.sb_off <= 229344, ("SBUF overflow", name, self.sb_off)
        self.nm += 1
        return self.nc.alloc_sbuf_tensor_at("%s_%d" % (name, self.nm), list(shape), dt, offset=off)

    def bank(self):
        i = self.psrr
        self.psrr = (self.psrr + 1) % 8
        return i

    def _deps(self, eng, reads, writes):
        deps = {}

        def add(tok):
            if tok is None:
                return
            s, v = tok
            if deps.get(s, 0) < v:
                deps[s] = v
        for k in reads:
            add(self.lastw.get(k))
        for k in writes:
            add(self.lastw.get(k))
            for r in self.readers.get(k, ()):
                add(r)
        waits = []
        own = self.esem[eng]
        for s, v in deps.items():
            if s == own and eng not in SAME_ENGINE_SYNC:
                continue
            if self.seen[eng].get(s, 0) < v:
                self.seen[eng][s] = v
                waits.append((s, v))
        return waits

    def _commit(self, tok, reads, writes):
        for k in reads:
            self.readers.setdefault(k, []).append(tok)
        for k in writes:
            self.lastw[k] = tok
            self.readers[k] = []

    def op(self, eng, fn, reads=(), writes=(), cls=None):
        waits = self._deps(eng, reads, writes)
        s = self.esem[eng if cls is None else cls]
        self.semval[s] += 1
        tok = (s, self.semval[s])
        self.ops[eng].append((waits, fn, s, 1))
        self._commit(tok, reads, writes)

    def dma(self, q, out, in_, reads=(), writes=()):
        waits = self._deps(q, reads, writes)
        lst = self.dq[q]
        s = lst[self.drr[q] % len(lst)]
        self.drr[q] += 1
        prev = self.semval[s]
        if prev > 0 and self.seen[q].get(s, 0) < prev:
            self.seen[q][s] = prev
            waits.append((s, prev))
        self.semval[s] += 16
        tok = (s, self.semval[s])
        self.ops[q].append((waits, lambda e, o=out, i=in_: e.dma_start(out=o, in_=i), s, 16))
        self._commit(tok, reads, writes)

    def barrier(self):
        for e in ENG:
            waits = []
            for s in range(len(self.sems)):
                v = self.semval[s]
                if v > 0 and self.seen[e].get(s, 0) < v:
                    self.seen[e][s] = v
                    waits.append((s, v))
            if waits:
                self.ops[e].append((waits, None, None, 0))
        self.lastw = {}
        self.readers = {}

    def _replay(self, e, eng):
        for waits, fn, s, inc in self.ops[e]:
            for ws, wv in waits:
                eng.wait_ge(self.sems[ws], wv)
            if fn is not None:
                ins = fn(eng)
                ins.then_inc(self.sems[s], inc)

    def emit(self):
        nc = self.nc
        with nc.Block() as block:
            @block.tensor
            def _(e):
                self._replay('pe', e)

            @block.scalar
            def _(e):
                self._replay('act', e)

            @block.vector
            def _(e):
                self._replay('dve', e)

            @block.gpsimd
            def _(e):
                self._replay('pool', e)

            @block.sync
            def _(e):
                self._replay('sp', e)
        return nc

    def mm(self, out, lhsT, rhs, start, stop, reads, writes):
        self.op('pe', lambda e: e.matmul(out, lhsT, rhs, start=start, stop=stop), reads, writes)

    def tr(self, out, in_, ident, reads, writes):
        self.op('pe', lambda e: e.transpose(out, in_, ident), reads, writes, cls='pe_tr')

    def act(self, out, in_, func, reads, writes, scale=None, bias=None, accum_out=None):
        kw = {}
        if scale is not None:
            kw['scale'] = scale
        if bias is not None:
            kw['bias'] = bias
        if accum_out is not None:
            kw['accum_out'] = accum_out
        self.op('act', lambda e: e.activation(out=out, in_=in_, func=func, **kw), reads, writes)

    def tt(self, eng, out, in0, in1, op, reads, writes):
        self.op(eng, lambda e: e.tensor_tensor(out=out, in0=in0, in1=in1, op=op), reads, writes)

    def ts(self, eng, out, in0, s1, op0, reads, writes, s2=None, op1=None):
        if op1 is None:
            self.op(eng, lambda e: e.tensor_scalar(out=out, in0=in0, scalar1=s1, scalar2=None, op0=op0), reads, writes)
        else:
            self.op(eng, lambda e: e.tensor_scalar(out=out, in0=in0, scalar1=s1, scalar2=s2, op0=op0, op1=op1), reads, writes)

    def stt(self, out, in0, scalar, in1, op0, op1, reads, writes):
        self.op('dve', lambda e: e.scalar_tensor_tensor(out=out, in0=in0, scalar=scalar, in1=in1, op0=op0, op1=op1),
                reads, writes)

    def cp(self, eng, out, in_, reads, writes):
        if eng == 'act':
            self.op('act', lambda e: e.copy(out=out, in_=in_), reads, writes)
        else:
            self.op(eng, lambda e: e.tensor_copy(out=out, in_=in_), reads, writes)

    def memset(self, eng, ap, val, writes):
        self.op(eng, lambda e: e.memset(ap, val), (), writes)


def build(debug=None):
    P = Prog()
    nc = P.nc

    def dram_in(name, shape, dt=F32):
        return nc.dram_tensor(name, list(shape), dt, kind="ExternalInput").ap()

    def dram_scr(name, shape, dt):
        return nc.dram_tensor(name, list(shape), dt, kind="Internal").ap()

    d_xT = dram_in("xT", [128, KC, U])
    d_cT = dram_in("cT", [128, KC, 2])
    d_wada = dram_in("w_ada", [L, 24, 128, KC * 128])
    d_bada = dram_in("b_adaT", [128, L * 24])
    d_ng = dram_in("norm_gT", [128, L * KC])
    d_fg = dram_in("final_gT", [128, KC])
    d_cw = dram_in("conv_wT", [128, L * KC * 5])
    d_cb = dram_in("conv_b", [L, 1, D])
    d_gb = dram_in("gate_bB", [128, L * 16])
    d_sink = dram_in("sink", [1, L * 8])
    d_hg = dram_in("head_gB", [L, 128, 512])
    d_wfm = dram_in("w_fm", [L, NFM, 128, KC * 128])
    d_wtm0 = dram_in("w_tm0", [L, 128, KC * 144])
    d_wtmh = dram_in("w_tmh", [L, 4, 128, KC * 384])
    d_woa = dram_in("w_out_a", [L, 8, 128, 4 * 128])
    d_wom = dram_in("w_out_m", [L, 8, 128, 4 * 128])
    d_ident = dram_in("ident", [128, 128])
    d_cos = dram_in("cosT", [128, T])
    d_sin = dram_in("sinT", [128, T])
    d_mle = dram_in("mask_le", [128, 128])
    d_mge = dram_in("mask_ge", [128, 128])
    d_nprev = dram_in("negprev", [128, 512])
    d_nnext = dram_in("negnext", [128, 512])
    d_sel = dram_in("sel", [128, 8])
    d_selden = dram_in("selden", [1, 256])
    d_out = nc.dram_tensor("outT", [128, KC, T], F32, kind="ExternalOutput").ap()

    s_q = dram_scr("s_q", [128, 4, U], BF16)
    s_k = dram_scr("s_k", [128, U], BF16)
    s_az = dram_scr("s_az", [128, 4, U], BF16)
    s_v = dram_scr("s_v", [128, NCH, 256], BF16)
    s_qm = dram_scr("s_qm", [4, 128, U], BF16)
    s_km = dram_scr("s_km", [4, 128, U], BF16)
    s_vv = dram_scr("s_vv", [4, 128, NCH, 129], BF16)
    s_to = dram_scr("s_to", [4, 128, NCH, 128], BF16)
    s_zg = dram_scr("s_zg", [4, 128, NCH, 128], BF16)

    dbg = {}
    if debug:
        for name, shape in debug.items():
            dbg[name] = nc.dram_tensor("dbg_" + name, list(shape), F32, kind="ExternalOutput").ap()

    xT = P.sb("xT", [128, KC, U], F32)
    cosT = P.sb("cosT", [128, T], F32)
    sinT = P.sb("sinT", [128, T], F32)
    identf = P.sb("identf", [128, 128], F32)
    identb = P.sb("identb", [128, 128], BF16)
    mle = P.sb("mle", [128, 128], F32)
    mge = P.sb("mge", [128, 128], F32)
    nprev = P.sb("nprev", [128, 512], BF16)
    nnext = P.sb("nnext", [128, 512], BF16)
    onesb = P.sb("onesb", [128, 128], BF16)
    onesf = P.sb("onesf", [128, 128], F32)
    onesrow = P.sb("onesrow", [1, 512], BF16)
    selden = P.sb("selden", [1, 256], BF16)
    sel = P.sb("sel", [128, 8], F32)
    cst = P.sb("cst", [128, 8], F32)
    badaT = P.sb("badaT", [128, L * 24], F32)
    ngs = P.sb("ngs", [128, L * KC], F32)
    fgs = P.sb("fgs", [128, KC], F32)
    cw = P.sb("cw", [128, L * KC * 5], F32)
    gbB = P.sb("gbB", [128, L * 16], F32)
    sinkrow = P.sb("sinkrow", [1, L * 8], F32)
    esf = P.sb("esf", [1, L * 8], F32)
    eshb = P.sb("eshb", [1, L * 8], BF16)
    eshf = P.sb("eshf", [1, L * 8], F32)
    eslf = P.sb("eslf", [1, L * 8], F32)
    esrow_hi = P.sb("esrow_hi", [1, 8 * 128], BF16)
    esrow_lo = P.sb("esrow_lo", [1, 8 * 128], BF16)
    cT = P.sb("cT", [128, KC * 2], F32)
    cth = P.sb("cth", [128, KC * 2], F32)
    sc2 = P.sb("sc2", [128, KC * 2], BF16)
    modT = P.sb("modT", [128, 48], F32)
    gs = P.sb("gs", [128, 16], F32)
    hgb = P.sb("hgb", [128, 512], F32)
    cbrow = P.sb("cbrow", [1, D], BF16)
    wo = [P.sb("wo%d" % i, [128, 512], BF16) for i in range(2)]
    gates = P.sb("gates", [128, NCH * 8], F32)
    nlfp = P.sb("nlfp", [128, NCH * 36], F32)
    apad = P.sb("apad", [128, NCH * 36], F32)
    fexp = P.sb("fexp", [128, NCH * 8], F32)
    nbs = P.sb("nbs", [128, NCH * 8], F32)
    ach = P.sb("ach", [128, NCH * 8], F32)
    Mb = P.sb("Mb", [128, NCH * 8], F32)
    c1b = P.sb("c1b", [128, NCH * 8], F32)
    ww = P.sb("ww", [128, NCH * 8], F32)
    Et = P.sb("Et", [128, NCH * 8], F32)
    tmpg = P.sb("tmpg", [128, NCH * 8], F32)
    mxT = P.sb("mxT", [128, NCH], F32)
    totT = P.sb("totT", [128, NCH], F32)
    Mt = P.sb("Mt", [128, NCH], F32)
    mt = P.sb("mt", [128, NCH], F32)
    dl = P.sb("dl", [128, NCH], F32)
    c1t = P.sb("c1t", [128, NCH], F32)
    RM = P.sb("RM", [128, NCH * 8], F32)
    RC = P.sb("RC", [128, NCH * 8], F32)
    persist_end = P.sb_off

    for kc in range(KC):
        P.dma('sp', xT[:, kc, :], d_xT[:, kc, :], (), [('xT', kc, g) for g in range(5)])
    for (sbt, dr, key) in [(cosT, d_cos, 'cos'), (sinT, d_sin, 'sin'), (identf, d_ident, 'identf'),
                           (mle, d_mle, 'mle'), (mge, d_mge, 'mge'), (sel, d_sel, 'sel'),
                           (badaT, d_bada, 'bada'), (ngs, d_ng, 'ngs'), (fgs, d_fg, 'fgs'), (cw, d_cw, 'cw'),
                           (gbB, d_gb, 'gbB'), (sinkrow, d_sink, 'sinkrow'), (cT, d_cT.rearrange("p a b -> p (a b)"), 'cT')]:
        P.dma('sp', sbt[:], dr, (), [key])
    for (sbt, dr, key) in [(identb, d_ident, 'identb'), (nprev, d_nprev, 'nprev'), (nnext, d_nnext, 'nnext'),
                           (selden, d_selden, 'selden')]:
        P.dma('pool', sbt[:], dr, (), [key])
    P.memset('pool', onesb[:], 1.0, ['onesb'])
    P.memset('pool', onesf[:], 1.0, ['onesf'])
    P.memset('pool', onesrow[:], 1.0, ['onesrow'])
    cvals = [0.0, 1.0, D * EPS, math.log(0.5), EPS * 4.0, math.log((128.0 ** -0.5) / 4.0), 0.0, 0.0]
    for i, v in enumerate(cvals):
        P.memset('pool', cst[:, i:i + 1], v, [('cst', i)])
    for t_ in (nlfp, apad, mxT, totT, Mt, mt, c1t, RM, RC):
        P.memset('pool', t_[:], 0.0, [t_.name])
    P.memset('pool', dl[:], -30000.0, ['dl'])
    K_NLFP, K_APAD, K_MXT, K_TOTT, K_MT, K_mt, K_C1T, K_RM, K_RC = [t_.name for t_ in (nlfp, apad, mxT, totT, Mt, mt, c1t, RM, RC)]
    P.ts('dve', ngs[:], ngs[:], 32.0, ALU.mult, ['ngs'], ['ngs'])
    P.ts('dve', fgs[:], fgs[:], 32.0, ALU.mult, ['fgs'], ['fgs'])
    P.act(cth[:], cT[:], AF.Tanh, ['cT'], ['cth'], scale=0.5)
    P.stt(cth[:], cth[:], 1.0, cT[:], ALU.add, ALU.mult, ['cth', 'cT'], ['cth'])
    P.ts('dve', sc2[:], cth[:], 0.5, ALU.mult, ['cth'], ['sc2'])
    P.act(esf[:], sinkrow[:], AF.Exp, ['sinkrow'], ['esf'])
    P.cp('dve', eshb[:], esf[:], ['esf'], ['eshb'])
    P.cp('dve', eshf[:], eshb[:], ['eshb'], ['eshf'])
    P.tt('dve', eslf[:], esf[:], eshf[:], ALU.subtract, ['esf', 'eshf'], ['eslf'])

    def bcast_last(ap2d, n):
        return ap2d.unsqueeze(2).to_broadcast([ap2d.shape[0], ap2d.shape[1], n])

    def xk(kc, g):
        return ('xT', kc, g)

    def dump(name, src_ap, key):
        if name in dbg:
            P.dma('sp', dbg[name], src_ap, [key], [('dbg', name)])

    for l in range(L):
        last = (l == L - 1)
        if l > 0:
            P.barrier()
            P.new_epoch()
        P.sb_off = persist_end
        ph1 = P.sb_off
        P.mark('L%d_ph0' % l)
        wa = [P.sb("wa%d" % i, [128, KC * 128], BF16) for i in range(3)]
        psm = P.bank()
        for cc in range(24):
            w = wa[cc % 3]
            wk = ('wa', cc % 3)
            P.dma('pool', w[:], d_wada[l, cc], (), [wk])
            for kc in range(KC):
                P.mm(P.ps[psm][:, 2 * cc:2 * cc + 2], w[:, kc * 128:(kc + 1) * 128], sc2[:, 2 * kc:2 * kc + 2],
                     kc == 0, kc == KC - 1, [wk, 'sc2'], [('ps', psm)])
        P.tt('dve', modT[:].rearrange("p (a b) -> p a b", b=2), P.ps[psm][:, 0:48].rearrange("p (a b) -> p a b", b=2),
             bcast_last(badaT[:, l * 24:(l + 1) * 24], 2), ALU.add, [('ps', psm), 'bada'], ['modT'])
        P.stt(gs[:].rearrange("p (a b) -> p a b", b=2), modT[:, 16:32].rearrange("p (a b) -> p a b", b=2), 1.0,
              bcast_last(ngs[:, l * KC:(l + 1) * KC], 2), ALU.add, ALU.mult, ['modT', 'ngs'], ['gs'])

        def shift_ap(kc, col):
            return modT[:, 2 * kc + col:2 * kc + col + 1]

        def gs_ap(kc, col):
            return gs[:, 2 * kc + col:2 * kc + col + 1]

        def gate_ap(kc, col):
            return modT[:, 32 + 2 * kc + col:32 + 2 * kc + col + 1]

        P.dma('sp', hgb[:], d_hg[l], (), ['hgb'])
        P.ts('dve', hgb[:], hgb[:], 0.5, ALU.mult, ['hgb'], ['hgb'])
        P.dma('pool', cbrow[:], d_cb[l], (), ['cbrow'])
        P.cp('dve', esrow_hi[:].rearrange("p (a b) -> p a b", b=128), bcast_last(eshb[:, l * 8:(l + 1) * 8], 128),
             ['eshb'], ['esrow_hi'])
        P.cp('dve', esrow_lo[:].rearrange("p (a b) -> p a b", b=128), bcast_last(eslf[:, l * 8:(l + 1) * 8], 128),
             ['eslf'], ['esrow_lo'])

        P.mark('L%d_ph1' % l)
        P.sb_off = ph1
        xnT = P.sb("xnT", [128, KC, U], BF16)
        sq = [P.sb("sq%d" % i, [128, 512], BF16) for i in range(3)]
        lnv = P.sb("lnv", [128, 512], F32)
        rstd = P.sb("rstd", [128, 512], F32)
        ut = [P.sb("ut%d" % i, [128, 512], F32) for i in range(3)]
        cnt = 0
        for g, (u0, n) in enumerate(GROUPS):
            col = 1 if g == 0 else 0
            pss = P.bank()
            for kc in range(KC):
                s_ = sq[cnt % 3]
                sk = ('sq', cnt % 3)
                src = xT[:, kc, u0:u0 + n]
                if cnt % 2 == 0:
                    P.act(s_[:, :n], src, AF.Square, [xk(kc, g)], [sk])
                else:
                    P.tt('pool', s_[:, :n], src, src, ALU.mult, [xk(kc, g)], [sk])
                cnt += 1
                P.mm(P.ps[pss][:, :n], onesb[:], s_[:, :n], kc == 0, kc == KC - 1, ['onesb', sk], [('ps', pss)])
            P.act(lnv[:, :n], P.ps[pss][:, :n], AF.Ln, [('ps', pss), ('cst', 2)], ['lnv'], bias=cst[:, 2:3])
            P.act(rstd[:, :n], lnv[:, :n], AF.Exp, ['lnv'], ['rstd'], scale=-0.5)
            for kc in range(KC):
                u_ = ut[kc % 3]
                uk = ('ut', kc % 3)
                P.tt('dve' if kc % 2 == 0 else 'pool', u_[:, :n], xT[:, kc, u0:u0 + n], rstd[:, :n], ALU.mult,
                     [xk(kc, g), 'rstd'], [uk])
                P.act(xnT[:, kc, u0:u0 + n], u_[:, :n], AF.Identity, [uk, 'gs', 'modT'], [('xn', g)],
                      scale=gs_ap(kc, col), bias=shift_ap(kc, col))

        P.mark('L%d_ph2' % l)
        wf = [P.sb("wf%d" % i, [128, KC * 128], BF16) for i in range(3)]
        wt = P.sb("wt", [128, KC * 384], BF16)
        raw = P.sb("raw", [128, U + 8], BF16)
        dg = P.sb("dg", [128, 40 * 128], BF16)
        tst = [P.sb("tst%d" % i, [128, 512], BF16) for i in range(2)]
        stg = sq
        t1 = [ut[0], ut[1]]
        t2 = [ut[2], lnv]
        T1K = [('ut', 0), ('ut', 1)]
        T2K = [('ut', 2), 'lnv']
        P.memset('pool', raw[:], 0.0, ['raw'])
        for i in range(40):
            P.ts('dve', dg[:, i * 128:(i + 1) * 128], identf[:], cw[:, l * 40 + i:l * 40 + i + 1], ALU.mult,
                 ['identf', 'cw'], [('dg', i)])
        sgc = [0]
        t1c = [0]

        def load_wf(ci):
            P.dma('pool', wf[ci % 3][:], d_wfm[l, ci], (), [('wf', ci % 3)])

        def proj_fm(ci, g):
            u0, n = GROUPS[g]
            b = P.bank()
            w = wf[ci % 3]
            for kc in range(KC):
                P.mm(P.ps[b][:, :n], w[:, kc * 128:(kc + 1) * 128], xnT[:, kc, u0:u0 + n], kc == 0, kc == KC - 1,
                     [('wf', ci % 3), ('xn', g)], [('ps', b)])
            return b

        def next_stg():
            i = sgc[0] % 3
            sgc[0] += 1
            return stg[i], ('sq', i)

        load_wf(0)
        load_wf(1)
        for pi in range(5):
            ci = 2 * pi
            if ci + 2 < NFM:
                load_wf(ci + 2)
            for g, (u0, n) in enumerate(GROUPS):
                b1 = proj_fm(ci, g)
                st_, sk_ = next_stg()
                if g == 0:
                    P.cp('act', st_[:, :n], P.ps[b1][:, :n], [('ps', b1)], [sk_])
                else:
                    b2 = proj_fm(ci + 1, g)
                    i = t1c[0] % 2
                    t1c[0] += 1
                    tt0 = u0 - LC
                    P.tt('dve', t1[i][:, :n], P.ps[b1][:, :n], cosT[:, tt0:tt0 + n], ALU.mult, [('ps', b1), 'cos'], [T1K[i]])
                    P.tt('dve', t2[i][:, :n], P.ps[b2][:, :n], sinT[:, tt0:tt0 + n], ALU.mult, [('ps', b2), 'sin'], [T2K[i]])
                    P.tt('pool', st_[:, :n], t1[i][:, :n], t2[i][:, :n], ALU.add, [T1K[i], T2K[i]], [sk_])
                dst = s_q[:, pi, u0:u0 + n] if pi < 4 else s_k[:, u0:u0 + n]
                dk = ('s_q', pi) if pi < 4 else 's_k'
                P.dma('sp', dst, st_[:, :n], [sk_], [dk])
            if ci + 3 < NFM:
                load_wf(ci + 3)
        P.mark('L%d_ph2az' % l)
        for j in range(4):
            ci = 10 + j
            if ci + 2 < NFM:
                load_wf(ci + 2)
            for g, (u0, n) in enumerate(GROUPS):
                b1 = proj_fm(ci, g)
                st_, sk_ = next_stg()
                i = t1c[0] % 2
                t1c[0] += 1
                P.act(t1[i][:, :n], P.ps[b1][:, :n], AF.Tanh, [('ps', b1)], [T1K[i]], scale=0.5)
                P.stt(st_[:, :n], t1[i][:, :n], 1.0, P.ps[b1][:, :n], ALU.add, ALU.mult, [T1K[i], ('ps', b1)], [sk_])
                P.dma('sp', s_az[:, j, u0:u0 + n], st_[:, :n], [sk_], [('s_az', j)])
        P.mark('L%d_ph2mqk' % l)
        for m in range(8):
            ci = 14 + m
            if ci + 2 < NFM:
                load_wf(ci + 2)
            for g, (u0, n) in enumerate(GROUPS):
                b1 = proj_fm(ci, g)
                r0 = u0 + 2 if g == 0 else u0 + 6
                P.cp('act', raw[:, r0:r0 + n], P.ps[b1][:, :n], [('ps', b1)], ['raw'])
            for g, (u0, n) in enumerate(GROUPS):
                r0 = u0 + 2 if g == 0 else u0 + 6
                b = P.bank()
                for jt in range(5):
                    P.mm(P.ps[b][:, :n], dg[:, (m * 5 + jt) * 128:(m * 5 + jt + 1) * 128],
                         raw[:, r0 + jt - 2:r0 + jt - 2 + n], jt == 0, False, [('dg', m * 5 + jt), 'raw'], [('ps', b)])
                P.mm(P.ps[b][:, :n], cbrow[0:1, m * 128:(m + 1) * 128], onesrow[0:1, :n], False, True,
                     ['cbrow', 'onesrow'], [('ps', b)])
                st_, sk_ = next_stg()
                i = t1c[0] % 2
                t1c[0] += 1
                P.act(t1[i][:, :n], P.ps[b][:, :n], AF.Tanh, [('ps', b)], [T1K[i]], scale=0.5)
                P.stt(st_[:, :n], t1[i][:, :n], 1.0, P.ps[b][:, :n], ALU.add, ALU.mult, [T1K[i], ('ps', b)], [sk_])
                if m < 4:
                    P.dma('sp', s_qm[m, :, u0:u0 + n], st_[:, :n], [sk_], [('s_qm', m)])
                else:
                    P.dma('sp', s_km[m - 4, :, u0:u0 + n], st_[:, :n], [sk_], [('s_km', m - 4)])
        P.mark('L%d_ph2tm0' % l)
        P.dma('pool', wt[:, 0:KC * 144], d_wtm0[l], (), ['wt'])
        tsc = [0]
        for c in range(NCH):
            b = P.bank()
            for kc in range(KC):
                P.mm(P.ps[b][:, 0:144], xnT[:, kc, c * 128:(c + 1) * 128], wt[:, kc * 144:(kc + 1) * 144],
                     kc == 0, kc == KC - 1, ['wt', ('xn', min(4, (c + 2) // 4))], [('ps', b)])
            i = tsc[0] % 2
            tsc[0] += 1
            vt = tst[i]
            vk = ('tst', i)
            P.memset('pool', vt[:, 0:256], 1.0, [vk])
            P.cp('act', vt[:, 0:64], P.ps[b][:, 0:64], [('ps', b)], [vk])
            P.cp('act', vt[:, 192:256], P.ps[b][:, 64:128], [('ps', b)], [vk])
            P.dma('sp', s_v[:, c, :], vt[:, 0:256], [vk], [('s_v', c)])
            P.tt('dve', gates[:, c * 8:(c + 1) * 8], P.ps[b][:, 128:136], gbB[:, l * 16:l * 16 + 8], ALU.add,
                 [('ps', b), 'gbB'], [('gates', c)])
            P.tt('dve', fexp[:, c * 8:(c + 1) * 8], P.ps[b][:, 136:144], gbB[:, l * 16 + 8:l * 16 + 16], ALU.add,
                 [('ps', b), 'gbB'], [('fexp', c)])

        P.mark('L%d_gate' % l)
        allg = [('gates', c) for c in range(NCH)]
        allf = [('fexp', c) for c in range(NCH)]
        P.act(fexp[:], fexp[:], AF.Exp, allf, ['fexp_e'], scale=-1.0)
        nl3 = nlfp[:].rearrange("p (c k) -> p c k", k=36)
        ap3 = apad[:].rearrange("p (c k) -> p c k", k=36)
        f3 = fexp[:].rearrange("p (c k) -> p c k", k=8)
        P.act(nl3[:, :, 0:4], f3[:, :, 0:4], AF.Ln, ['fexp_e', ('cst', 1)], [K_NLFP], bias=cst[:, 1:2])
        P.act(nl3[:, :, 32:36], f3[:, :, 4:8], AF.Ln, ['fexp_e', ('cst', 1)], [K_NLFP], bias=cst[:, 1:2])
        bnb = P.bank()
        for c in range(NCH):
            P.mm(P.ps[bnb][:, c * 8:c * 8 + 4], mle[:], nlfp[:, c * 36:c * 36 + 4], True, True, ['mle', K_NLFP], [('ps', bnb)])
            P.mm(P.ps[bnb][:, c * 8 + 4:c * 8 + 8], mge[:], nlfp[:, c * 36 + 32:c * 36 + 36], True, True,
                 ['mge', K_NLFP], [('ps', bnb)])
        P.cp('dve', nbs[:], P.ps[bnb][:, 0:NCH * 8], [('ps', bnb)], ['nbs'])
        P.tt('dve', ach[:], gates[:], nbs[:], ALU.add, allg + ['nbs'], ['ach'])
        a3 = ach[:].rearrange("p (c k) -> p c k", k=8)
        P.cp('dve', ap3[:, :, 0:4], a3[:, :, 0:4], ['ach'], [K_APAD])
        P.cp('dve', ap3[:, :, 32:36], a3[:, :, 4:8], ['ach'], [K_APAD])
        for c0 in range(0, NCH, 4):
            nc4 = min(4, NCH - c0)
            b = P.bank()
            for c in range(c0, c0 + nc4):
                P.mm(P.ps[b][0:36, (c - c0) * 128:(c - c0 + 1) * 128], apad[:, c * 36:(c + 1) * 36], identf[:], True, True,
                     [K_APAD, 'identf'], [('ps', b)])
            P.op('dve', lambda e, b=b, c0=c0, nc4=nc4: e.tensor_reduce(
                out=mxT[0:36, c0:c0 + nc4], in_=P.ps[b][0:36, 0:nc4 * 128].rearrange("p (c k) -> p c k", k=128),
                axis=AX.X, op=ALU.max), [('ps', b)], [K_MXT])
        btot = P.bank()
        for c in range(NCH):
            P.mm(P.ps[btot][0:36, c:c + 1], nlfp[:, c * 36:(c + 1) * 36], onesf[:, 0:1], True, True,
                 [K_NLFP, 'onesf'], [('ps', btot)])
        P.cp('dve', totT[0:36, :], P.ps[btot][0:36, 0:NCH], [('ps', btot)], [K_TOTT])
        for d in range(2):
            pp = slice(32 * d, 32 * d + 4)
            kM, km_, kd = ('Mt', d), ('mt', d), ('dl', d)
            for j, c in enumerate(ORDER[d]):
                if j == 0:
                    P.cp('dve', Mt[pp, c:c + 1], mxT[pp, c:c + 1], [K_MXT, K_MT], [kM])
                else:
                    cp_ = ORDER[d][j - 1]
                    P.tt('dve', Mt[pp, c:c + 1], mt[pp, cp_:cp_ + 1], mxT[pp, c:c + 1], ALU.max, [km_, K_MXT, K_MT], [kM])
                    P.tt('dve', dl[pp, c:c + 1], mt[pp, cp_:cp_ + 1], Mt[pp, c:c + 1], ALU.subtract, [km_, kM, 'dl'], [kd])
                P.tt('dve', mt[pp, c:c + 1], Mt[pp, c:c + 1], totT[pp, c:c + 1], ALU.subtract, [kM, K_TOTT, K_mt], [km_])
        P.act(c1t[0:36, :], dl[0:36, :], AF.Exp, [('dl', 0), ('dl', 1), 'dl', K_C1T], ['c1t_v'])
        sel3 = sel[0:36, :].unsqueeze(1).to_broadcast([36, NCH, 8])
        P.tt('dve', RM[0:36, :].rearrange("p (c k) -> p c k", k=8), bcast_last(Mt[0:36, :], 8), sel3, ALU.mult,
             [('Mt', 0), ('Mt', 1), 'sel', K_RM], ['RM_v'])
        P.tt('dve', RC[0:36, :].rearrange("p (c k) -> p c k", k=8), bcast_last(c1t[0:36, :], 8), sel3, ALU.mult,
             ['c1t_v', 'sel', K_RC], ['RC_v'])
        bM = P.bank()
        P.mm(P.ps[bM][:, 0:NCH * 8], onesf[0:36, :], RM[0:36, :], True, True, ['onesf', 'RM_v'], [('ps', bM)])
        P.mm(P.ps[bM][:, 256:256 + NCH * 8], onesf[0:36, :], RC[0:36, :], True, True, ['onesf', 'RC_v'], [('ps', bM)])
        P.cp('dve', Mb[:], P.ps[bM][:, 0:NCH * 8], [('ps', bM)], ['Mb'])
        P.cp('dve', c1b[:], P.ps[bM][:, 256:256 + NCH * 8], [('ps', bM)], ['c1b'])
        P.tt('dve', tmpg[:], ach[:], Mb[:], ALU.subtract, ['ach', 'Mb'], ['tmpg'])
        P.act(ww[:], tmpg[:], AF.Exp, ['tmpg', ('cst', 5)], ['ww'], bias=cst[:, 5:6])
        P.tt('dve', tmpg[:], nbs[:], Mb[:], ALU.subtract, ['nbs', 'Mb', 'ww'], ['tmpg'])
        P.act(Et[:], tmpg[:], AF.Exp, ['tmpg'], ['Et'])

        P.mark('L%d_ph2tmh' % l)
        for h in range(4):
            P.dma('pool', wt[:, 0:KC * 384], d_wtmh[l, h], (), ['wt'])
            for c in range(NCH):
                b = P.bank()
                for kc in range(KC):
                    P.mm(P.ps[b][:, 0:384], xnT[:, kc, c * 128:(c + 1) * 128], wt[:, kc * 384:(kc + 1) * 384],
                         kc == 0, kc == KC - 1, ['wt', ('xn', min(4, (c + 2) // 4))], [('ps', b)])
                i = tsc[0] % 2
                tsc[0] += 1
                vt = tst[i]
                vk = ('tst', i)
                P.memset('pool', vt[:, 128:129], 1.0, [vk])
                P.cp('act', vt[:, 0:128], P.ps[b][:, 0:128], [('ps', b)], [vk])
                P.act(vt[:, 129:257], P.ps[b][:, 128:256], AF.Tanh, [('ps', b)], [vk], scale=0.5)
                j_ = t1c[0] % 2
                t1c[0] += 1
                P.act(t1[j_][:, 0:128], P.ps[b][:, 256:384], AF.Tanh, [('ps', b)], [T1K[j_]], scale=0.5)
                P.stt(vt[:, 257:385], t1[j_][:, 0:128], 1.0, P.ps[b][:, 256:384], ALU.add, ALU.mult,
                      [T1K[j_], ('ps', b)], [vk])
                P.dma('sp', s_vv[h, :, c, :], vt[:, 0:129], [vk], [('s_vv', h)])
                P.dma('sp', s_to[h, :, c, :], vt[:, 129:257], [vk], [('s_to', h)])
                P.dma('sp', s_zg[h, :, c, :], vt[:, 257:385], [vk], [('s_zg', h)])

        P.mark('L%d_ph3' % l)
        P.barrier()
        P.sb_off = ph1
        qT = P.sb("qT", [128, 4, U], BF16)
        kT = P.sb("kT", [128, U], BF16)
        Va = P.sb("Va", [128, NCH, 256], BF16)
        azT = P.sb("azT", [128, 4, U], BF16)
        PT = [P.sb("PT%d" % i, [128, 512], BF16) for i in range(8)]
        lnd = [P.sb("lnd%d" % i, [128, 512], F32) for i in range(2)]
        rdn = [P.sb("rdn%d" % i, [128, 512], F32) for i in range(2)]
        tnm = [P.sb("tnm%d" % i, [128, 512], F32) for i in range(2)]
        for j in range(4):
            P.dma('sp', qT[:, j, :], s_q[:, j, :], [('s_q', j)], ['qT'])
            P.dma('sp', azT[:, j, :], s_az[:, j, :], [('s_az', j)], [('azT', j, g) for g in range(5)])
        P.dma('sp', kT[:], s_k[:], ['s_k'], ['kT'])
        P.dma('sp', Va[:], s_v[:], [('s_v', c) for c in range(NCH)], ['Va'])
        lc = [0]
        qblocks = list(range(2, NCH)) + ([] if last else [0, 1])
        tasks = []
        units = []
        for qc in qblocks:
            if qc >= 2:
                n_ = qc - 2
                ktiles = []
                if n_ > 0:
                    ktiles.append((qc - 1, nprev))
                ktiles.append((qc, None))
                if n_ < 15:
                    ktiles.append((qc + 1, nnext))
                ktiles += [(0, None), (1, None)]
            else:
                ktiles = [(0, None), (1, None)]
            ub = len(units)
            units.append((qc, 0))
            units.append((qc, 1))
            for ti, (kc_, msk) in enumerate(ktiles):
                for g in range(2):
                    tasks.append((ub + g, kc_, msk, ti == 0, ti == len(ktiles) - 1))
        NT = len(tasks)
        LA = 3
        SBK = [0, 1, 2, 3]
        POK = [4, 5, 6, 7]
        for k in range(NT + LA):
            if k < NT:
                u_, kc_, msk, first, lastt = tasks[k]
                qc, g = units[u_]
                qu0 = qc * 128
                pr = slice(64 * g, 64 * g + 64)
                b = SBK[k % 4]
                P.mm(P.ps[b][:].rearrange("p (j q) -> p j q", q=128), kT[pr, kc_ * 128:(kc_ + 1) * 128],
                     qT[pr, :, qu0:qu0 + 128], True, msk is None, ['kT', 'qT'], [('ps', b)])
                if msk is not None:
                    P.mm(P.ps[b][:], identb[:], msk[:], False, True, ['identb', 'nprev', 'nnext'], [('ps', b)])
                P.act(PT[k % 8][:], P.ps[b][:], AF.Exp, [('ps', b)], [('PT', k % 8)], scale=0.125)
            k2 = k - LA
            if k2 >= 0:
                u_, kc_, msk, first, lastt = tasks[k2]
                qc, g = units[u_]
                qu0 = qc * 128
                qg = min(4, (qc + 2) // 4)
                pr = slice(64 * g, 64 * g + 64)
                dr = slice(64 - 64 * g, 128 - 64 * g)
                bo = POK[u_ % 4]
                P.mm(P.ps[bo][:], Va[:, kc_, g * 128:(g + 1) * 128], PT[k2 % 8][:], first, False, ['Va', ('PT', k2 % 8)], [('ps', bo)])
                if lastt:
                    P.mm(P.ps[bo][:], selden[0:1, g * 128:(g + 1) * 128], esrow_hi[0:1, g * 512:(g + 1) * 512], False, False,
                         ['selden', 'esrow_hi'], [('ps', bo)])
                    P.mm(P.ps[bo][:], selden[0:1, g * 128:(g + 1) * 128], esrow_lo[0:1, g * 512:(g + 1) * 512], False, True,
                         ['selden', 'esrow_lo'], [('ps', bo)])
                    i2 = lc[0] % 2
                    lc[0] += 1
                    P.act(lnd[i2][pr, :], P.ps[bo][dr, :], AF.Ln, [('ps', bo)], [('lnd', i2)])
                    P.act(rdn[i2][pr, :], lnd[i2][pr, :], AF.Exp, [('lnd', i2), ('cst', 3)], [('rdn', i2)], scale=-1.0, bias=cst[pr, 3:4])
                    P.tt('dve', tnm[i2][pr, :], P.ps[bo][pr, :], rdn[i2][pr, :], ALU.mult, [('ps', bo), ('rdn', i2)], [('tnm', i2)])
                    P.tt('pool', azT[pr, :, qu0:qu0 + 128], tnm[i2][pr, :].rearrange("p (j q) -> p j q", q=128),
                         azT[pr, :, qu0:qu0 + 128], ALU.mult, [('tnm', i2)] + [('azT', j, qg) for j in range(4)],
                         [('azT', j, qg) for j in range(4)])
        P.mark('L%d_ph3o' % l)
        for dc in range(8):
            w = wo[dc % 2]
            P.dma('pool', w[:], d_woa[l, dc], (), [('wo', dc % 2)])
            for g, (u0, n) in enumerate(GROUPS):
                if last and g == 0:
                    continue
                col = 1 if g == 0 else 0
                b = P.bank()
                for j in range(4):
                    P.mm(P.ps[b][:, :n], w[:, j * 128:(j + 1) * 128], azT[:, j, u0:u0 + n], j == 0, j == 3,
                         [('wo', dc % 2), ('azT', j, g)], [('ps', b)])
                P.stt(xT[:, dc, u0:u0 + n], P.ps[b][:, :n], gate_ap(dc, col), xT[:, dc, u0:u0 + n], ALU.mult, ALU.add,
                      [('ps', b), 'modT', xk(dc, g)], [xk(dc, g)])

        P.mark('L%d_ph4' % l)
        P.barrier()
        P.sb_off = ph1
        mT = P.sb("mT", [128, 4, U], BF16)
        qm = [P.sb("qm%d" % i, [128, U], BF16) for i in range(2)]
        km = [P.sb("km%d" % i, [128, U], BF16) for i in range(2)]
        vv = [P.sb("vv%d" % i, [128, NCH, 129], BF16) for i in range(2)]
        tho = [P.sb("tho%d" % i, [128, NCH, 128], BF16) for i in range(2)]
        zg = [P.sb("zg%d" % i, [128, NCH, 128], BF16) for i in range(2)]
        hacc1 = P.sb("hacc", [128, NCH, 128], F32)
        hacc = [hacc1, hacc1]
        Cst = [[P.sb("Cst%d%d" % (a, b_), [128, 129], F32) for b_ in range(2)] for a in range(2)]
        Crb = [[P.sb("Crb%d%d" % (a, b_), [128, 129], BF16) for b_ in range(2)] for a in range(2)]
        PTm = [P.sb("PTm%d" % i, [128, 128], BF16) for i in range(4)]
        kwm = [P.sb("kwm%d" % i, [128, 128], BF16) for i in range(4)]
        dn = [P.sb("dn%d" % i, [128, 2], F32) for i in range(8)]
        hgt = [P.sb("hgt%d" % i, [128, 128], F32) for i in range(4)]
        zgh = [P.sb("zgh%d" % i, [128, 128], F32) for i in range(4)]
        sqj = [P.sb("sqj%d" % i, [128, 128], F32) for i in range(2)]
        ssr = [P.sb("ssr%d" % i, [128, 4], F32) for i in range(8)]
        mtk = [P.sb("mtk%d" % i, [128, 128], BF16) for i in range(4)]
        ABK = [0]
        BBK = [1]
        OUK = [2, 3, 4, 5]
        UBK = 6
        TBK = 7

        def cgrp(c):
            return min(4, (c + 2) // 4)

        def load_head(h, i):
            ks = [('qm', i, g) for g in range(5)]
            P.dma('sp', qm[i][:], s_qm[h], [('s_qm', h)], [('qm', i, g) for g in range(5)])
            P.dma('sp', km[i][:], s_km[h], [('s_km', h)], [('km', i, g) for g in range(5)])
            P.dma('sp', vv[i][:], s_vv[h], [('s_vv', h)], [('vv', i, g) for g in range(5)])
            P.dma('sp', tho[i][:], s_to[h], [('s_to', h)], [('tho', i, g) for g in range(5)])
            P.dma('sp', zg[i][:], s_zg[h], [('s_zg', h)], [('zg', i, g) for g in range(5)])

        J1 = {c: ORDER[1].index(c) for c in range(NCH)}
        pcnt = [0]
        for pair in range(1):
            load_head(0, 0)
            steps = [(h_, j, d) for h_ in range(4) for j in range(NCH) for d in range(2)]
            NS = len(steps)
            info = {}
            for s_, (h_, j, d) in enumerate(steps):
                c = ORDER[d][j]
                h = h_
                hs = h_ % 2
                other = J1[c] if d == 0 else c
                info[s_] = dict(hs=hs, j=j, d=d, c=c, h=h, col=d * 4 + h, g=cgrp(c), second=(other < j),
                                need_h=not (last and c < 2), csl=slice(c * 128, (c + 1) * 128))

            def S1(s_):
                I = info[s_]
                hs, c, g, col, csl = I['hs'], I['c'], I['g'], I['col'], I['csl']
                ab = ABK[0]
                r = s_ % 4
                wcol = ww[:, c * 8 + col:c * 8 + col + 1]
                P.mm(P.ps[ab][:, 0:128], km[hs][:, csl], qm[hs][:, csl], True, True, [('km', hs, g), ('qm', hs, g)], [('ps', ab)])
                bb = BBK[0]
                psb16 = P.ps[bb][:].bitcast(BF16)
                P.tr(psb16[:, 0:128], km[hs][:, csl], identb[:], [('km', hs, g), 'identb'], [('ps', bb)])
                P.stt(PTm[r][:], P.ps[ab][:, 0:128], wcol, (mle if I['d'] == 0 else mge)[:], ALU.mult, ALU.mult,
                      [('ps', ab), 'ww', 'mle', 'mge'], [('PTm', r)])
                P.act(kwm[r][:], psb16[:, 0:128], AF.Identity, [('ps', bb), 'ww'], [('kwm', r)], scale=wcol)

            def S2(s_):
                I = info[s_]
                hs, c, g, col, csl, j, d = I['hs'], I['c'], I['g'], I['col'], I['csl'], I['j'], I['d']
                ou = OUK[s_ % 4]
                r = s_ % 4
                if j > 0:
                    P.mm(P.ps[ou][:, 0:129], qm[hs][:, csl], Crb[hs][d][:], True, False, [('qm', hs, g), ('Crb', hs, d)], [('ps', ou)])
                P.mm(P.ps[ou][:, 0:129], PTm[r][:], vv[hs][:, c, :], j == 0, True, [('PTm', r), ('vv', hs, g)], [('ps', ou)])
                P.mm(P.ps[UBK][:, 0:129], kwm[r][:], vv[hs][:, c, :], True, True, [('kwm', r), ('vv', hs, g)], [('ps', UBK)])

            def S2a(s_):
                I = info[s_]
                hs, c, col, j, d = I['hs'], I['c'], I['col'], I['j'], I['d']
                if j > 0:
                    P.act(Crb[hs][d][:], Cst[hs][d][:], AF.Identity, [('Cst', hs, d), 'c1b'], [('Crb', hs, d)],
                          scale=c1b[:, c * 8 + col:c * 8 + col + 1])

            def S3a(s_):
                I = info[s_]
                hs, c, col, j, d = I['hs'], I['c'], I['col'], I['j'], I['d']
                ou = OUK[s_ % 4]
                q = s_ % 8
                P.tt('dve', dn[q][:, 1:2], P.ps[ou][:, 128:129], Et[:, c * 8 + col:c * 8 + col + 1], ALU.max,
                     [('ps', ou), 'Et'], [('dn', q)])
                if j == 0:
                    P.cp('dve', Cst[hs][d][:], P.ps[UBK][:, 0:129], [('ps', UBK)], [('Cst', hs, d)])
                else:
                    P.stt(Cst[hs][d][:], Cst[hs][d][:], c1b[:, c * 8 + col:c * 8 + col + 1], P.ps[UBK][:, 0:129], ALU.mult, ALU.add,
                          [('Cst', hs, d), 'c1b', ('ps', UBK)], [('Cst', hs, d)])

            def S3b(s_):
                ou = OUK[s_ % 4]
                q = s_ % 8
                P.stt(dn[q][:, 0:1], P.ps[ou][:, 128:129], -1.0, dn[q][:, 1:2], ALU.mult, ALU.max,
                      [('ps', ou), ('dn', q)], [('dn', q)])

            def S3c(s_):
                q = s_ % 8
                P.op('dve', lambda e, q=q, dn=dn: e.reciprocal(out=dn[q][:, 1:2], in_=dn[q][:, 0:1]), [('dn', q)], [('dn', q)])

            def S3d(s_):
                I = info[s_]
                hs, c = I['hs'], I['c']
                ou = OUK[s_ % 4]
                q = s_ % 8
                if not I['need_h']:
                    return
                if not I['second']:
                    P.act(hacc[hs][:, c, :], P.ps[ou][:, 0:128], AF.Identity, [('ps', ou), ('dn', q)], [('hacc', c)],
                          scale=dn[q][:, 1:2])
                else:
                    P.stt(hacc[hs][:, c, :], P.ps[ou][:, 0:128], dn[q][:, 1:2], hacc[hs][:, c, :], ALU.mult, ALU.add,
                          [('ps', ou), ('dn', q), ('hacc', c)], [('hacc', c)])
                    I['p'] = pcnt[0]
                    pcnt[0] += 1

            def post_ok(s_):
                return 'p' in info[s_]

            def S3e(s_):
                if not post_ok(s_):
                    return
                I = info[s_]
                hs, c, g, h = I['hs'], I['c'], I['g'], I['h']
                p = I['p']
                P.stt(hgt[p % 4][:], tho[hs][:, c, :], 1.0, hacc[hs][:, c, :], ALU.add, ALU.mult,
                      [('tho', hs, g), ('hacc', c)], [('hgt', p % 4)])
                P.tt('pool', zgh[p % 4][:], zg[hs][:, c, :], hgb[:, h * 128:(h + 1) * 128], ALU.mult,
                     [('zg', hs, g), 'hgb'], [('zgh', p % 4)])

            def S3f(s_):
                if not post_ok(s_):
                    return
                p = info[s_]['p']
                P.act(sqj[p % 2][:], hgt[p % 4][:], AF.Square, [('hgt', p % 4)], [('sqj', p % 2), ('ssr', p % 8)],
                      accum_out=ssr[p % 8][:, 0:1])

            def S3g(s_):
                if not post_ok(s_):
                    return
                p = info[s_]['p']
                P.act(ssr[p % 8][:, 1:2], ssr[p % 8][:, 0:1], AF.Ln, [('ssr', p % 8), ('cst', 4)], [('ssr', p % 8)],
                      scale=1.0 / 128.0, bias=cst[:, 4:5])

            def S3h(s_):
                if not post_ok(s_):
                    return
                p = info[s_]['p']
                P.act(ssr[p % 8][:, 2:3], ssr[p % 8][:, 1:2], AF.Exp, [('ssr', p % 8)], [('ssr', p % 8)], scale=-0.5)

            def S3i(s_):
                if not post_ok(s_):
                    return
                p = info[s_]['p']
                P.stt(mtk[p % 4][:], hgt[p % 4][:], ssr[p % 8][:, 2:3], zgh[p % 4][:], ALU.mult, ALU.mult,
                      [('hgt', p % 4), ('ssr', p % 8), ('zgh', p % 4)], [('mtk', p % 4)])

            def S3j(s_):
                if not post_ok(s_):
                    return
                p = info[s_]['p']
                pst16 = P.ps[TBK][:].bitcast(BF16)
                P.tr(pst16[:, 0:128], mtk[p % 4][:], identb[:], [('mtk', p % 4), 'identb'], [('ps', TBK)])

            def S3k(s_):
                if not post_ok(s_):
                    return
                I = info[s_]
                pst16 = P.ps[TBK][:].bitcast(BF16)
                P.cp('act', mT[:, I['h'], I['csl']], pst16[:, 0:128], [('ps', TBK)], [('mT', I['h'], I['g'])])

            def S3all(s_):
                for f_ in (S3a, S3b, S3c, S3d):
                    f_(s_)

            def Spost(s_):
                for f_ in (S3e, S3f, S3g, S3h, S3i, S3j, S3k):
                    f_(s_)

            stages = [S1, S2, S3all, Spost]
            def S3bc(s_):
                S3b(s_)
                S3c(s_)

            sched = [(S2a, 1), (S1, 0), (S3a, 2), (S3bc, 3), (S3d, 4), (S2, 1), (Spost, 5)]
            stages = [None] * 7
            for it in range(NS + len(stages)):
                if it >= len(stages) and (it - len(stages)) % (2 * NCH) == 0:
                    hn = (it - len(stages)) // (2 * NCH) + 1
                    if hn < 4:
                        load_head(hn, hn % 2)
                for fn_, k_ in sched:
                    s_ = it - k_
                    if 0 <= s_ < NS:
                        fn_(s_)
        P.mark('L%d_ph5' % l)
        for dc in range(8):
            w = wo[dc % 2]
            P.dma('pool', w[:], d_wom[l, dc], (), [('wo', dc % 2)])
            for g, (u0, n) in enumerate(GROUPS):
                if last and g == 0:
                    continue
                col = 1 if g == 0 else 0
                b = P.bank()
                for hh in range(4):
                    P.mm(P.ps[b][:, :n], w[:, hh * 128:(hh + 1) * 128], mT[:, hh, u0:u0 + n], hh == 0, hh == 3,
                         [('wo', dc % 2), ('mT', hh, g)], [('ps', b)])
                P.stt(xT[:, dc, u0:u0 + n], P.ps[b][:, :n], gate_ap(dc, col), xT[:, dc, u0:u0 + n], ALU.mult, ALU.add,
                      [('ps', b), 'modT', xk(dc, g)], [xk(dc, g)])

    P.mark('final')
    P.barrier()
    P.new_epoch()
    P.sb_off = persist_end
    sq = [P.sb("fsq%d" % i, [128, 512], BF16) for i in range(3)]
    lnv = P.sb("flnv", [128, 512], F32)
    rstd = P.sb("frstd", [128, 512], F32)
    ot = [P.sb("fot%d" % i, [128, 512], F32) for i in range(3)]
    cnt = 0
    oc = 0
    for g in range(1, 5):
        u0, n = GROUPS[g]
        pss = P.bank()
        for kc in range(KC):
            s_ = sq[cnt % 3]
            sk = ('fsq', cnt % 3)
            src = xT[:, kc, u0:u0 + n]
            if cnt % 2 == 0:
                P.act(s_[:, :n], src, AF.Square, [xk(kc, g)], [sk])
            else:
                P.tt('pool', s_[:, :n], src, src, ALU.mult, [xk(kc, g)], [sk])
            cnt += 1
            P.mm(P.ps[pss][:, :n], onesb[:], s_[:, :n], kc == 0, kc == KC - 1, ['onesb', sk], [('ps', pss)])
        P.act(lnv[:, :n], P.ps[pss][:, :n], AF.Ln, [('ps', pss), ('cst', 2)], ['flnv'], bias=cst[:, 2:3])
        P.act(rstd[:, :n], lnv[:, :n], AF.Exp, ['flnv'], ['frstd'], scale=-0.5)
        for kc in range(KC):
            o_ = ot[oc % 3]
            ok = ('fot', oc % 3)
            oc += 1
            P.stt(o_[:, :n], xT[:, kc, u0:u0 + n], fgs[:, kc:kc + 1], rstd[:, :n], ALU.mult, ALU.mult,
                  [xk(kc, g), 'fgs', 'frstd'], [ok])
            P.dma('sp', d_out[:, kc, u0 - LC:u0 - LC + n], o_[:, :n], [ok], [('out', kc, g)])
    P.barrier()
    P.emit()
    return nc, P


_CONST_CACHE = {}


def _consts():
    if _CONST_CACHE:
        return _CONST_CACHE
    f32 = np.float32
    ident = np.eye(128, dtype=f32)
    s = np.arange(128)[:, None]
    t = np.arange(128)[None, :]
    mask_le = (s <= t).astype(f32)
    mask_ge = (s >= t).astype(f32)
    NEG = -30000.0
    negprev = np.where(t > s, NEG, 0.0).astype(f32)
    negnext = np.where(s > t, NEG, 0.0).astype(f32)
    negprev = np.tile(negprev, (1, 4))
    negnext = np.tile(negnext, (1, 4))
    sel = np.zeros((128, 8), f32)
    for i in range(4):
        sel[i, i] = 1.0
        sel[32 + i, 4 + i] = 1.0
    selden = np.zeros((1, 256), f32)
    selden[0, 64:128] = 1.0
    selden[0, 128:192] = 1.0
    half = 32
    inv = (10000.0 ** (-np.arange(0, half, 2, dtype=f32) / half)).astype(f32)
    tt_ = np.arange(T)
    row = (tt_ // 64).astype(f32)
    colp = (tt_ % 64).astype(f32)
    cosT = np.zeros((128, T), f32)
    sinT = np.zeros((128, T), f32)
    for p in range(128):
        f = p % 64
        pos = row if f < 32 else colp
        ang = (pos * inv[f % 16]).astype(f32)
        cosT[p] = np.cos(ang).astype(f32)
        sgn = -1.0 if (f % 32) < 16 else 1.0
        sinT[p] = (sgn * np.sin(ang)).astype(f32)
    _CONST_CACHE.update(dict(ident=ident, mask_le=mask_le, mask_ge=mask_ge, negprev=negprev, negnext=negnext,
                             sel=sel, selden=selden, cosT=cosT, sinT=sinT))
    return _CONST_CACHE


def _kchunk(w):
    n = w.shape[1]
    return np.ascontiguousarray(w.reshape(KC, 128, n).transpose(1, 0, 2).reshape(128, KC * n))


def _prep_shared(w_ada, b_ada, norm_g, w_in, conv_w, conv_b, gate_b, sink, head_g, w_out, final_g):
    f32 = np.float32
    a64 = np.arange(64)
    sw = np.where((a64 % 32) < 16, a64 + 16, a64 - 16)
    fm_cols = []
    for j in range(4):
        fm_cols.append(np.concatenate([j * 64 + a64, (4 + j) * 64 + a64]))
        fm_cols.append(np.concatenate([j * 64 + sw, (4 + j) * 64 + sw]))
    fm_cols.append(512 + np.arange(128))
    fm_cols.append(512 + np.concatenate([sw, 64 + sw]))
    for j in range(4):
        fm_cols.append(768 + np.concatenate([j * 64 + a64, (4 + j) * 64 + a64]))
    for h in range(4):
        fm_cols.append(1280 + h * 128 + np.arange(128))
    for h in range(4):
        fm_cols.append(1792 + h * 128 + np.arange(128))
    assert len(fm_cols) == NFM
    tm0 = np.concatenate([640 + np.arange(128), 3840 + np.arange(16)])
    tmh = [np.concatenate([2304 + h * 128 + np.arange(128), 2816 + h * 128 + np.arange(128),
                           3328 + h * 128 + np.arange(128)]) for h in range(4)]
    w_fm = np.empty((L, NFM, 128, KC * 128), f32)
    w_tm0 = np.empty((L, 128, KC * 144), f32)
    w_tmh = np.empty((L, 4, 128, KC * 384), f32)
    w_ada_l = np.empty((L, 24, 128, KC * 128), f32)
    w_out_a = np.empty((L, 8, 128, 512), f32)
    w_out_m = np.empty((L, 8, 128, 512), f32)
    for l in range(L):
        for ci, cols in enumerate(fm_cols):
            w_fm[l, ci] = _kchunk(w_in[l][:, cols])
        w_tm0[l] = _kchunk(w_in[l][:, tm0])
        for h in range(4):
            w_tmh[l, h] = _kchunk(w_in[l][:, tmh[h]])
        for cc in range(24):
            w_ada_l[l, cc] = _kchunk(w_ada[l][:, cc * 128:(cc + 1) * 128])
        for dc in range(8):
            for j in range(4):
                rows = np.concatenate([j * 64 + a64, (4 + j) * 64 + a64])
                w_out_a[l, dc, :, j * 128:(j + 1) * 128] = w_out[l][rows, dc * 128:(dc + 1) * 128]
                rows_m = 512 + j * 128 + np.arange(128)
                w_out_m[l, dc, :, j * 128:(j + 1) * 128] = w_out[l][rows_m, dc * 128:(dc + 1) * 128]
    b_adaT = np.ascontiguousarray(b_ada.reshape(L, 24, 128).transpose(2, 0, 1).reshape(128, L * 24))
    norm_gT = np.ascontiguousarray(norm_g.reshape(L, KC, 128).transpose(2, 0, 1).reshape(128, L * KC))
    final_gT = np.ascontiguousarray(final_g.reshape(KC, 128).T)
    conv_wT = np.ascontiguousarray(conv_w.reshape(L, 5, KC, 128).transpose(3, 0, 2, 1).reshape(128, L * KC * 5))
    conv_b_r = np.ascontiguousarray(conv_b.reshape(L, 1, D))
    gate_bB = np.ascontiguousarray(np.broadcast_to(gate_b.reshape(1, L * 16), (128, L * 16)))
    sink_r = np.ascontiguousarray(sink.reshape(1, L * 8))
    head_gB = np.ascontiguousarray(np.broadcast_to(head_g.reshape(L, 1, 512), (L, 128, 512)))
    d = dict(w_ada=w_ada_l, b_adaT=b_adaT, norm_gT=norm_gT, final_gT=final_gT, conv_wT=conv_wT, conv_b=conv_b_r,
             gate_bB=gate_bB, sink=sink_r, head_gB=head_gB, w_fm=w_fm, w_tm0=w_tm0, w_tmh=w_tmh,
             w_out_a=w_out_a, w_out_m=w_out_m)
    d.update(_consts())
    return {k: np.ascontiguousarray(v, dtype=np.float32) for k, v in d.items()}


def _prep_core(x_b, ctx_b, c_b, c_ctx):
    xa = np.concatenate([ctx_b, x_b], axis=0)
    xT = np.ascontiguousarray(xa.T.reshape(KC, 128, U).transpose(1, 0, 2))
    cc = np.stack([c_b, c_ctx], axis=-1)
    cT = np.ascontiguousarray(cc.reshape(KC, 128, 2).transpose(1, 0, 2))
    return dict(xT=xT.astype(np.float32), cT=cT.astype(np.float32))


def kernel(x, c, ctx, c_ctx, w_ada, b_ada, norm_g, w_in, conv_w, conv_b, gate_b, sink, head_g, w_out, final_g,
           _debug=None, _cores=8):
    arrs = [np.asarray(a, dtype=np.float32) for a in
            (x, c, ctx, c_ctx, w_ada, b_ada, norm_g, w_in, conv_w, conv_b, gate_b, sink, head_g, w_out, final_g)]
    x, c, ctx, c_ctx, w_ada, b_ada, norm_g, w_in, conv_w, conv_b, gate_b, sink, head_g, w_out, final_g = arrs
    shared = _prep_shared(w_ada, b_ada, norm_g, w_in, conv_w, conv_b, gate_b, sink, head_g, w_out, final_g)
    nc, P = build(_debug)
    in_maps = []
    for b in range(_cores):
        m = dict(shared)
        m.update(_prep_core(x[b], ctx[b], c[b], c_ctx))
        in_maps.append(m)
    res = run_bass_kernel_spmd(nc, in_maps, core_ids=list(range(_cores)))
    outs = []
    for b in range(_cores):
        oT = res.results[b]["outT"]
        outs.append(np.ascontiguousarray(oT.transpose(1, 0, 2).reshape(D, T).T))
    out = np.stack(outs, axis=0).astype(np.float32)
    if _debug:
        return out, res
    return out
```
